# Optimizing a Trainium2 kernel written in Bass

```python
import jax, jax.numpy as jnp
from jax import lax
import numpy as np

D_MODEL = 4096
BATCH = 1
SEQ = 8192
DEPTH = 1

MIX_WIDTH = D_MODEL
CONV_WIDTH = MIX_WIDTH // 2
HG_WIDTH = MIX_WIDTH - CONV_WIDTH
HG_EXPAND = 128
HG_HEADS = HG_WIDTH // HG_EXPAND
HG_HEAD_DIM = HG_WIDTH // HG_HEADS
CONV_KSIZE = 31
CONV_GROUPS = 16
CHUNK = 64
D_FF = ((8 * D_MODEL // 3 + 255) // 256) * 256
IN_COLS = 2 * CONV_WIDTH + 4 * HG_WIDTH
SPLITS = tuple(int(s) for s in np.cumsum([CONV_WIDTH, CONV_WIDTH, HG_WIDTH, HG_WIDTH, HG_WIDTH]))
EPS = 1e-6

kernel_name = "hymba_conformer_hgrn2_block"


def rmsnorm(x, g):
    xf = x.astype(jnp.float32)
    y = xf * lax.rsqrt(jnp.mean(xf * xf, axis=-1, keepdims=True) + EPS)
    return (y * g.astype(jnp.float32)).astype(x.dtype)


def layernorm(x, g, b):
    xf = x.astype(jnp.float32)
    mu = jnp.mean(xf, axis=-1, keepdims=True)
    var = jnp.mean(jnp.square(xf - mu), axis=-1, keepdims=True)
    y = (xf - mu) * lax.rsqrt(var + EPS)
    return (y * g.astype(jnp.float32) + b.astype(jnp.float32)).astype(x.dtype)


def conformer_conv(xv, xg, w_dw, b_dw, ln_g, ln_b):
    h = xv * jax.nn.sigmoid(xg)
    hp = jnp.pad(h, ((0, 0), (CONV_KSIZE - 1, 0), (0, 0)))
    y = lax.conv_general_dilated(
        hp, w_dw[:, None, :].astype(h.dtype), window_strides=(1,), padding='VALID',
        dimension_numbers=('NWC', 'WIO', 'NWC'), feature_group_count=CONV_WIDTH)
    y = y + b_dw
    y = layernorm(y, ln_g, ln_b)
    return jax.nn.silu(y)


def hgrn2_mixer(q, f_logit, i, g, lb, norm_g):
    B, S, _ = q.shape
    n_chunks = S // CHUNK
    f32 = jnp.float32
    lbf = lb.astype(f32)
    qf = jax.nn.silu(q.astype(f32))
    f = lbf + (1.0 - lbf) * jax.nn.sigmoid(f_logit.astype(f32))
    k = 1.0 - f
    log_f = jnp.log(f)

    def to_chunks(t, d):
        return t.reshape(B, n_chunks, CHUNK, HG_HEADS, d).transpose(1, 0, 3, 2, 4)

    qc = to_chunks(qf, HG_EXPAND)
    kc = to_chunks(k, HG_EXPAND)
    vc = to_chunks(i.astype(f32), HG_HEAD_DIM)
    bc = jnp.cumsum(to_chunks(log_f, HG_EXPAND), axis=3)
    causal = jnp.tril(jnp.ones((CHUNK, CHUNK), dtype=bool))[:, :, None]

    def step(s_prev, inp):
        qj, kj, vj, bj = inp
        diff = bj[:, :, :, None, :] - bj[:, :, None, :, :]
        decay = jnp.exp(jnp.where(causal, diff, -jnp.inf))
        scores = jnp.einsum('bhtsk,bhsk->bhts', qj[:, :, :, None, :] * decay, kj)
        o = (jnp.einsum('bhts,bhsv->bhtv', scores, vj)
             + jnp.einsum('bhtk,bhkv->bhtv', qj * jnp.exp(bj), s_prev))
        b_last = bj[:, :, -1:, :]
        s_new = (jnp.exp(b_last[:, :, 0, :])[..., None] * s_prev
                 + jnp.einsum('bhsk,bhsv->bhkv', kj * jnp.exp(b_last - bj), vj))
        return s_new, o

    s0 = jnp.zeros((B, HG_HEADS, HG_EXPAND, HG_HEAD_DIM), f32)
    _, oc = lax.scan(step, s0, (qc, kc, vc, bc))
    o = oc.transpose(1, 0, 3, 2, 4).reshape(B, S, HG_HEADS, HG_HEAD_DIM)
    o = rmsnorm(o, norm_g.reshape(HG_HEADS, HG_HEAD_DIM))
    o = o.reshape(B, S, HG_WIDTH) * jax.nn.silu(g.astype(f32))
    return o.astype(q.dtype)


def setup_inputs(seed: int = 0) -> dict:
    key = jax.random.key(seed)
    ks = jax.random.split(key, 16)
    f32 = jnp.float32
    nrm = lambda k, shape, scale: jax.random.normal(k, shape, f32) * scale
    return {
        "x": nrm(ks[0], (BATCH, SEQ, D_MODEL), 1.0),
        "attn_norm_g": 1.0 + nrm(ks[1], (DEPTH, D_MODEL), 0.02),
        "w_in": nrm(ks[2], (DEPTH, D_MODEL, IN_COLS), D_MODEL ** -0.5),
        "conv_w": nrm(ks[3], (DEPTH, CONV_KSIZE, CONV_WIDTH), CONV_KSIZE ** -0.5),
        "conv_b": nrm(ks[4], (DEPTH, CONV_WIDTH), 0.02),
        "conv_ln_g": 1.0 + nrm(ks[5], (DEPTH, CONV_WIDTH), 0.02),
        "conv_ln_b": nrm(ks[6], (DEPTH, CONV_WIDTH), 0.02),
        "hg_lb_logits": nrm(ks[7], (DEPTH + 1, HG_WIDTH), 0.5),
        "hg_norm_g": 1.0 + nrm(ks[8], (DEPTH, HG_WIDTH), 0.02),
        "w_out": nrm(ks[9], (DEPTH, MIX_WIDTH, D_MODEL), MIX_WIDTH ** -0.5),
        "ffn_norm_g": 1.0 + nrm(ks[10], (DEPTH, D_MODEL), 0.02),
        "w_gate": nrm(ks[11], (DEPTH, D_MODEL, D_FF), D_MODEL ** -0.5),
        "w_up": nrm(ks[12], (DEPTH, D_MODEL, D_FF), D_MODEL ** -0.5),
        "w_down": nrm(ks[13], (DEPTH, D_FF, D_MODEL), D_FF ** -0.5),
        "final_norm_g": 1.0 + nrm(ks[14], (D_MODEL,), 0.02),
    }


def reference(x, attn_norm_g, w_in, conv_w, conv_b, conv_ln_g, conv_ln_b, hg_lb_logits,
              hg_norm_g, w_out, ffn_norm_g, w_gate, w_up, w_down, final_norm_g):
    lb_all = jnp.cumsum(jax.nn.softmax(hg_lb_logits.astype(jnp.float32), axis=0), axis=0)
    for l in range(DEPTH):
        h = rmsnorm(x, attn_norm_g[l])
        proj = jnp.einsum('bsd,dc->bsc', h, w_in[l])
        xv, xg, q, f_logit, i, g = jnp.split(proj, SPLITS, axis=-1)
        a_out = conformer_conv(xv, xg, conv_w[l], conv_b[l], conv_ln_g[l], conv_ln_b[l])
        b_out = hgrn2_mixer(q, f_logit, i, g, lb_all[l], hg_norm_g[l])
        mix = jnp.concatenate([a_out, b_out], axis=-1)
        x = x + jnp.einsum('bsm,md->bsd', mix, w_out[l])
        h = rmsnorm(x, ffn_norm_g[l])
        u = jax.nn.silu(jnp.einsum('bsd,df->bsf', h, w_gate[l])) * jnp.einsum('bsd,df->bsf', h, w_up[l])
        x = x + jnp.einsum('bsf,fd->bsd', u, w_down[l])
    return rmsnorm(x, final_norm_g)
```

```python
import numpy as np
from contextlib import ExitStack
import concourse.bass as bass
import concourse.mybir as mybir
from concourse.bass_utils import run_bass_kernel_spmd

F32 = mybir.dt.float32
BF16 = mybir.dt.bfloat16
AF = mybir.ActivationFunctionType
ALU = mybir.AluOpType

D = 4096
DC = D // 128
SEQ = 8192
NCORES = 8
T = 512
TT = T // 128
NH = 2
HALO = 32
TW = HALO + T
CW = 2048
NCB = CW // 128
HEADS = 16
DFF = 11008
NFG = DFF // 256
INC = 12288
KS = 31
EPS = 1e-6
NPR = 40


class Buf:
    __slots__ = ("name", "w", "r")

    def __init__(self, name):
        self.name = name
        self.w = None
        self.r = {}


class Prog:
    ENG = ("pe", "act", "dve", "pool", "sp")
    LIMIT = 2000

    def __init__(self, nc, es):
        self.nc = nc
        self.es = es
        self.ops = {e: [] for e in self.ENG}
        self.cnt = {}
        self.sems = {}
        self.epoch = {}

    def _sem(self, key):
        if key not in self.sems:
            self.sems[key] = self.es.enter_context(self.nc.semaphore("s_" + key))
            self.cnt[key] = 0
        return self.sems[key]

    def op(self, eng, fns, reads=(), writes=(), dma=None, extra=(), cc=None):
        if callable(fns):
            fns = [fns]
        waits = {}

        def add(tok):
            if tok is None:
                return
            k, v = tok
            if waits.get(k, 0) < v:
                waits[k] = v

        for b in reads:
            add(b.w)
        for b in writes:
            add(b.w)
            for k, v in b.r.items():
                add((k, v))
        for t in extra:
            add(t)
        if cc is not None:
            base, n_inc, inc, allinc = "c_" + cc, 1, None, False
        elif dma is not None:
            base, n_inc, inc, allinc = "d_" + dma, 16 * len(fns), 16, True
        else:
            base, n_inc, inc, allinc = eng, 1, 1, False
        ep = self.epoch.get(base, 0)
        key = f"{base}_{ep}"
        self._sem(key)
        if self.cnt[key] + n_inc > self.LIMIT:
            ep += 1
            self.epoch[base] = ep
            key = f"{base}_{ep}"
            self._sem(key)
        self.cnt[key] += n_inc
        tok = (key, self.cnt[key])
        for b in reads:
            if b.r.get(key, 0) < tok[1]:
                b.r[key] = tok[1]
        for b in writes:
            b.w = tok
            b.r = {}
        self.ops[eng].append((waits, fns, key, inc, allinc))
        return tok

    def replay(self, eng, e):
        seen = {}
        for waits, fns, key, inc, allinc in self.ops[eng]:
            for k, v in waits.items():
                if seen.get(k, 0) >= v:
                    continue
                e.wait_ge(self.sems[k], v)
                seen[k] = v
            n = len(fns)
            for i, f in enumerate(fns):
                inst = f(e)
                if allinc or i == n - 1:
                    if inc is None:
                        inst.then_inc(self.sems[key])
                    else:
                        inst.then_inc(self.sems[key], inc)


def build_nc(do_attn=True, nh=NH, nfg=NFG, do_ffn=True, small_w=False, skip=(), stop_after=None):
    nc = bass.Bass("TRN2", target_bir_lowering=False)
    xs = nc.dram_tensor("xs", [NH * TW, D], F32, kind="ExternalInput").ap()
    w_in = nc.dram_tensor("w_in", [D, INC], F32, kind="ExternalInput").ap()
    w_out = nc.dram_tensor("w_out", [D, D], F32, kind="ExternalInput").ap()
    w_gate = nc.dram_tensor("w_gate", [128, 128] if small_w else [D, DFF], F32, kind="ExternalInput").ap()
    w_up = nc.dram_tensor("w_up", [128, 128] if small_w else [D, DFF], F32, kind="ExternalInput").ap()
    w_down = nc.dram_tensor("w_down", [128, 128] if small_w else [DFF, D], F32, kind="ExternalInput").ap()
    prm = nc.dram_tensor("prm", [NPR, CW], F32, kind="ExternalInput").ap()
    gns = nc.dram_tensor("gns", [96, 128], F32, kind="ExternalInput").ap()
    fng = nc.dram_tensor("fng", [1, D], F32, kind="ExternalInput").ap()
    cident = nc.dram_tensor("cident", [128, 128], F32, kind="ExternalInput").ap()
    cmask = nc.dram_tensor("cmask", [128, 128], F32, kind="ExternalInput").ap()
    selm_d = nc.dram_tensor("selm", [128, 32], F32, kind="ExternalInput").ap()
    GR = 17 * 128
    gin = nc.dram_tensor("gin", [2 * GR, 128], F32)
    gout = nc.dram_tensor("gout", [NCORES * 2 * GR, 128], F32)
    out = nc.dram_tensor("out", [NH * T, D], F32, kind="ExternalOutput").ap()

    es = ExitStack()
    with es:
        P = Prog(nc, es)
        off = [0]

        sbc = {}

        def sb(name, shape, dt, at=None):
            if name in sbc:
                return sbc[name]
            sbc[name] = _sb(name, shape, dt, at)
            return sbc[name]

        def _sb(name, shape, dt, at=None):
            if at is None:
                at = off[0]
                nb = int(np.prod(shape[1:])) * (4 if dt == F32 else 2)
                off[0] = at + ((nb + 31) // 32) * 32
            return nc.alloc_sbuf_tensor_at(name, shape, dt, offset=16512 + at)

        PC = sb("PC", [128, NCB, NPR], F32)
        GN = sb("GN", [128, 96], F32)
        identb = sb("identb", [128, 128], BF16)
        identf = sb("identf", [128, 128], F32)
        maskb = sb("maskb", [128, TT, 128], BF16)
        onesf = sb("onesf", [128, 128], F32)
        stat = sb("stat", [128, 16], F32)
        lbv = sb("lbv", [128, NCB, 2], F32)
        dvec = sb("dvec", [128, 128], F32)
        dvec2 = sb("dvec2", [128, 128], F32)
        selm = sb("selm_sb", [128, 32], F32)
        assert off[0] <= 8192, off[0]
        OFF_HT = 8192
        hT = sb("hT", [128, DC, TW], BF16, at=OFF_HT)
        OFF_SP = OFF_HT + DC * T * 2
        OFF_RING = OFF_HT + DC * TW * 2
        SLOT = 16384
        ring = [sb(f"ring{i}", [128, SLOT // 2], BF16, at=OFF_RING + i * SLOT) for i in range(4)]
        OFF_X1 = OFF_RING + 4 * SLOT
        x1 = sb("x1", [128, TT, D], F32, at=OFF_X1)
        OFF_AB = OFF_X1 + TT * D * 4
        ring += [sb(f"ring{i}", [128, SLOT // 2], BF16, at=OFF_AB + (i - 4) * SLOT) for i in (4, 5)]
        OFF_U = OFF_AB + 2 * SLOT
        ubuf = [sb(f"u{i}", [128, 2, T], BF16, at=OFF_U + i * 2048) for i in range(2)]
        assert 16512 + OFF_U + 4096 <= 229344
        h2T = sb("h2T", [128, DC, T], BF16, at=OFF_HT)
        sgt = [sb(f"sgt{i}", [128, T], BF16, at=OFF_SP + i * 1024) for i in range(2)]
        hb_main = sb("hb", [128, D], BF16, at=OFF_AB + SLOT)
        xstage = sb("xstage", [128, D], F32, at=OFF_RING)
        hb2 = sb("hb2", [128, D], BF16, at=OFF_RING + SLOT)
        gb = sb("gb", [128, D], F32, at=OFF_RING)
        junk = sb("junk", [128, D], BF16, at=OFF_RING + SLOT)

        ps = [es.enter_context(nc.psum_tensor(f"ps{i}", [128, 512], F32)) for i in range(8)]

        B = {}

        def buf(name):
            if name not in B:
                B[name] = Buf(name)
            return B[name]

        bps = [buf(f"ps{i}") for i in range(8)]
        bring = [buf(f"ring{i}") for i in range(6)]

        P.op("sp", lambda e: e.dma_start(out=identf[:, :], in_=cident[:, :]), writes=[buf("identf")], dma="c0")
        P.op("sp", lambda e: e.dma_start(out=onesf[:, :], in_=cmask[:, :]), writes=[buf("onesf")], dma="c1")
        P.op("dve", lambda e: e.tensor_copy(out=identb[:, :], in_=identf[:, :]), reads=[buf("identf")], writes=[buf("identb")])
        P.op("dve", [lambda e, t=t: e.tensor_copy(out=maskb[:, t, :], in_=onesf[:, :]) for t in range(TT)],
             reads=[buf("onesf")], writes=[buf("maskb")])
        P.op("dve", lambda e: e.memset(onesf[:, :], 1.0), reads=[buf("maskb")], writes=[buf("onesf")])
        P.op("sp", lambda e: e.dma_start(out=selm[:, :], in_=selm_d[:, :]), writes=[buf("selm")], dma="c3")
        P.op("dve", lambda e: e.memset(dvec[:, :], 0.0), writes=[buf("dvec")])
        P.op("dve", lambda e: e.memset(dvec2[:, :], 0.0), writes=[buf("dvec2")])
        gstage = sb("gstage", [96, 128], F32, at=OFF_X1)
        P.op("sp", lambda e: e.dma_start(out=gstage[:, :], in_=gns[:, :]), writes=[buf("x1_0")], dma="x0")
        P.op("pe", lambda e: e.transpose(out=ps[0][:, 0:96], in_=gstage[:, :], identity=identf[0:96, 0:96]),
             reads=[buf("x1_0"), buf("identf")], writes=[bps[0]])
        P.op("dve", lambda e: e.tensor_copy(out=GN[:, :], in_=ps[0][:, 0:96]), reads=[bps[0]], writes=[buf("GN")])

        def ffn_ring_state():
            return {"i": 0}

        def norm_transpose(src_tile_ap, rows, dstT, tok0, gcol, srcbufs, dstbuf, tag, hbt=None):
            st = buf("stat")
            hb, bhb = (hb_main, bring[5]) if hbt is None else hbt
            P.op("act", lambda e: e.activation(out=hb[0:rows, :], in_=src_tile_ap, func=AF.Square,
                                               accum_out=stat[0:rows, 0:1]),
                 reads=srcbufs, writes=[bhb, st])
            P.op("dve", lambda e: e.tensor_scalar(out=stat[0:rows, 1:2], in0=stat[0:rows, 0:1], scalar1=1.0 / D,
                                                  scalar2=EPS, op0=ALU.mult, op1=ALU.add),
                 reads=[st], writes=[st])
            P.op("act", lambda e: e.activation(out=stat[0:rows, 2:3], in_=stat[0:rows, 1:2], func=AF.Sqrt),
                 reads=[st], writes=[st])
            P.op("dve", lambda e: e.reciprocal(out=stat[0:rows, 3:4], in_=stat[0:rows, 2:3]), reads=[st], writes=[st])
            P.op("act", lambda e: e.activation(out=hb[0:rows, :], in_=src_tile_ap, func=AF.Copy,
                                               scale=stat[0:rows, 3:4]),
                 reads=srcbufs + [st], writes=[bhb])
            for g8 in range(DC // 8):
                pb = bps[4 + (g8 % 2)]
                pst = ps[4 + (g8 % 2)].bitcast(BF16)
                P.op("pe", [lambda e, j=j, g8=g8, pst=pst: e.transpose(
                    out=pst[:, j * 128:j * 128 + rows], in_=hb[0:rows, (g8 * 8 + j) * 128:(g8 * 8 + j + 1) * 128],
                    identity=identb[0:rows, 0:rows]) for j in range(8)],
                    reads=[bhb, buf("identb")], writes=[pb])
                src3 = pst[:, :].rearrange("p (j t) -> p j t", t=128)[:, :, 0:rows]
                g3 = GN[:, gcol + g8 * 8: gcol + g8 * 8 + 8].unsqueeze(2).to_broadcast([128, 8, rows])
                P.op("dve", lambda e, g8=g8, src3=src3, g3=g3: e.tensor_tensor(
                    out=dstT[:, g8 * 8:g8 * 8 + 8, tok0:tok0 + rows], in0=src3, in1=g3, op=ALU.mult),
                    reads=[pb, buf("GN")], writes=[dstbuf])

        ring_i = [0]

        def next_slot():
            i = ring_i[0]
            ring_i[0] = (i + 1) % 6
            return i

        ring4_i = [0]

        def next_slot4():
            i = ring4_i[0]
            ring4_i[0] = (i + 1) % 4
            return i

        def next_pair4():
            i = ring4_i[0]
            if i % 2 == 1:
                i = (i + 1) % 4
            ring4_i[0] = (i + 2) % 4
            return i, i + 1

        class _Stop(Exception):
            pass

        def cut(name):
            if stop_after == name:
                raise _Stop()

        def half(hf):
            bx1 = [buf(f"x1_{t}") for t in range(TT)]
            def load_x_tiles():
                for t in range(TT):
                    r0 = hf * TW + HALO + t * 128
                    P.op("sp", lambda e, t=t, r0=r0: e.dma_start(out=x1[:, t, :], in_=xs[r0:r0 + 128, :]),
                         writes=[bx1[t]], dma=f"x{t}")

            if not do_attn:
                load_x_tiles()

            if do_attn:
                def fence(src, dst):
                    for d_ in dst:
                        for s_ in src:
                            if s_.w is not None and d_.r.get(s_.w[0], 0) < s_.w[1]:
                                d_.r[s_.w[0]] = s_.w[1]
                            for k_, v_ in s_.r.items():
                                if d_.r.get(k_, 0) < v_:
                                    d_.r[k_] = v_

                XO = OFF_X1
                y = sb("y", [128, NCB, T], F32, at=XO)
                oloc = sb("oloc", [128, HEADS, T], BF16, at=XO)
                gsT = sb("gsT", [128, HEADS, T], BF16, at=XO + 16384)
                TO = XO + 32768
                sgc = [sb(f"sgc{i}", [128, TW], F32, at=TO + i * 2176) for i in range(2)]
                hgc = [sb(f"hgc{i}", [128, TW], F32, at=TO + 4352 + i * 2176) for i in range(2)]
                sqt = [sb(f"sqt{i}", [128, T], F32, at=TO + 8704 + i * 2048) for i in range(2)]
                mt = sb("mt", [128, T], F32, at=TO + 12800)
                rstdt = sb("rstdt", [128, T], F32, at=TO + 14848)
                nmrt = sb("nmrt", [128, T], F32, at=TO + 16896)
                sgm = sb("sgm", [128, T], F32, at=TO)
                fch = sb("fch", [128, T], F32, at=TO + 2048)
                pch = sb("pch", [128, T], F32, at=TO + 4096)
                pseg = sb("pseg", [128, T], F32, at=TO + 6144)
                rp = sb("rp", [128, T], F32, at=TO + 8192)
                qs = sb("qs", [128, T], F32, at=TO + 10240)
                qt = sb("qt", [128, T], BF16, at=TO + 12288)
                kt = sb("kt", [128, T], BF16, at=TO + 13312)
                ktok = sb("ktok", [128, T], BF16, at=TO + 14336)
                vtok = sb("vtok", [128, TT, 128], BF16, at=TO + 15360)
                scm = sb("scm", [128, TT, 128], BF16, at=TO + 16384)
                Sst = [sb(f"Sst{i}", [128, 128], F32, at=TO + 17408 + i * 512) for i in range(2)]
                tmpS = sb("tmpS", [128, 128], F32, at=TO + 18432)
                Sb = sb("Sb", [128, 8, 128], BF16, at=TO + 18944)
                vE = sb("vE", [128, TT, 128], BF16, at=TO + 20992)
                vO = sb("vO", [128, TT, 128], BF16, at=TO + 22016)
                of_ = sb("of_", [128, T], F32, at=TO)
                sqf = sb("sqf", [128, T], F32, at=TO + 2048)
                rsf = sb("rsf", [128, T], F32, at=TO + 4096)
                Lr = [sb(f"Lr{i}", [128, HEADS, 128], F32, at=TO + i * 8192) for i in range(2)]
                Sacc = sb("Sacc", [128, HEADS, 128], F32, at=TO + 16384)
                dall = [sb(f"dall{i}", [128, NCORES, HEADS], F32, at=TO + 24576 + i * 512) for i in range(2)]
                deff = sb("deff", [128, HEADS], F32, at=TO + 25600)
                SinB = sb("SinB", [128, HEADS, 128], BF16, at=TO + 25664)
                a_out = sb("a_out", [128, NCB, T], BF16, at=OFF_AB)
                b_out = sb("b_out", [128, HEADS, T], BF16, at=OFF_AB + SLOT)
                xh = sb("xh", [32, D], F32, at=OFF_AB)
                pstage = sb("pstage", [NPR, CW], F32, at=OFF_X1 + 16384)

                if hf == 0:
                    P.op("sp", lambda e: e.dma_start(out=pstage[:, :], in_=prm[:, :]), writes=[bx1[1]], dma="x1")
                    for half8 in range(2):
                        P.op("pe", [lambda e, cb=cb, half8=half8: e.transpose(
                            out=ps[1 + half8][:, (cb % 8) * NPR:(cb % 8 + 1) * NPR], in_=pstage[:, cb * 128:(cb + 1) * 128],
                            identity=identf[0:NPR, 0:NPR]) for cb in range(half8 * 8, half8 * 8 + 8)],
                            reads=[bx1[1], buf("identf")], writes=[bps[1 + half8]])
                        P.op("dve", lambda e, half8=half8: e.tensor_copy(
                            out=PC[:, half8 * 8:half8 * 8 + 8, :],
                            in_=ps[1 + half8][:, 0:8 * NPR].rearrange("p (c r) -> p c r", r=NPR)),
                            reads=[bps[1 + half8]], writes=[buf("PC")])
                    P.op("dve", lambda e: e.tensor_tensor(out=lbv[:, :, 0], in0=PC[:, :, 34], in1=PC[:, :, 35], op=ALU.subtract),
                         reads=[buf("PC")], writes=[buf("lbv")])
                    P.op("act", lambda e: e.activation(out=lbv[:, :, 0], in_=lbv[:, :, 0], func=AF.Sigmoid),
                         reads=[buf("lbv")], writes=[buf("lbv")])
                    P.op("dve", lambda e: e.tensor_scalar(out=lbv[:, :, 1], in0=lbv[:, :, 0], scalar1=-1.0, scalar2=1.0,
                                                          op0=ALU.mult, op1=ALU.add), reads=[buf("lbv")], writes=[buf("lbv")])

                cut("params")
                load_x_tiles()
                P.op("sp", lambda e, hf=hf: e.dma_start(out=xh[:, :], in_=xs[hf * TW:hf * TW + HALO, :]),
                     writes=[bring[4]], dma="xh")
                bhT = buf("hT")
                norm_transpose(xh[:, :], HALO, hT, 0, 0, [bring[4]], bhT, "h")
                for t in range(TT):
                    norm_transpose(x1[:, t, :], 128, hT, HALO + t * 128, 0, [bx1[t]], bhT, "a")

                cut("a0")
                by = [buf(f"y{cb}") for cb in range(NCB)]
                ba = [buf(f"a{cb}") for cb in range(NCB)]
                bsgc = [buf("sgc0"), buf("sgc1")]
                bhgc = [buf("hgc0"), buf("hgc1")]
                bsqt = [buf("sqt0"), buf("sqt1")]
                ctemps = by + bsgc + bhgc + bsqt + [buf("mt"), buf("rstdt"), buf("nmrt")]
                fence(bx1, ctemps)
                fence([bring[4]], ba)
                for cg in range(NCB // 2):
                    sv_, sg2_ = next_slot4(), next_slot4()
                    vsl = ring[sv_][:, :].rearrange("p (dc c) -> p dc c", c=256)
                    gsl2 = ring[sg2_][:, :].rearrange("p (dc c) -> p dc c", c=256)
                    P.op("pool", lambda e, cg=cg, vsl=vsl: e.dma_start(
                        out=vsl, in_=w_in[:, cg * 256:(cg + 1) * 256].rearrange("(dc p) c -> p dc c", p=128)),
                        writes=[bring[sv_]], dma=f"r{sv_}")
                    P.op("pool", lambda e, cg=cg, gsl2=gsl2: e.dma_start(
                        out=gsl2, in_=w_in[:, CW + cg * 256:CW + (cg + 1) * 256].rearrange("(dc p) c -> p dc c", p=128)),
                        writes=[bring[sg2_]], dma=f"r{sg2_}")
                    for cbl in range(2):
                        cb = 2 * cg + cbl
                        pA, pB, pC = (0, 1, 2) if cb % 2 == 0 else (3, 4, 5)
                        i2 = cb % 2
                        P.op("pe", [lambda e, dc=dc, cbl=cbl, pA=pA, vsl=vsl: e.matmul(
                            ps[pA][:, :], lhsT=vsl[:, dc, cbl * 128:(cbl + 1) * 128], rhs=hT[:, dc, HALO:TW],
                            start=(dc == 0), stop=(dc == DC - 1)) for dc in range(DC)],
                            reads=[bring[sv_], bhT], writes=[bps[pA]])
                        P.op("pe", [lambda e, dc=dc, cbl=cbl, pB=pB, gsl2=gsl2: e.matmul(
                            ps[pB][:, :], lhsT=gsl2[:, dc, cbl * 128:(cbl + 1) * 128], rhs=hT[:, dc, HALO:TW],
                            start=(dc == 0), stop=(dc == DC - 1)) for dc in range(DC)],
                            reads=[bring[sg2_], bhT], writes=[bps[pB]])
                        P.op("pe", [lambda e, dc=dc, cbl=cbl, pC=pC, vsl=vsl: e.matmul(
                            ps[pC][:, 0:HALO], lhsT=vsl[:, dc, cbl * 128:(cbl + 1) * 128], rhs=hT[:, dc, 0:HALO],
                            start=(dc == 0), stop=(dc == DC - 1)) for dc in range(DC)] +
                            [lambda e, dc=dc, cbl=cbl, pC=pC, gsl2=gsl2: e.matmul(
                            ps[pC][:, HALO:2 * HALO], lhsT=gsl2[:, dc, cbl * 128:(cbl + 1) * 128], rhs=hT[:, dc, 0:HALO],
                            start=(dc == 0), stop=(dc == DC - 1)) for dc in range(DC)],
                            reads=[bring[sv_], bring[sg2_], bhT], writes=[bps[pC]])
                        P.op("act", [lambda e, pB=pB, i2=i2: e.activation(out=sgc[i2][:, HALO:TW], in_=ps[pB][:, :], func=AF.Sigmoid),
                                     lambda e, pC=pC, i2=i2: e.activation(out=sgc[i2][:, 0:HALO], in_=ps[pC][:, HALO:2 * HALO], func=AF.Sigmoid)],
                             reads=[bps[pB], bps[pC]], writes=[bsgc[i2]])
                        P.op("dve", [lambda e, pA=pA, i2=i2: e.tensor_tensor(out=hgc[i2][:, HALO:TW], in0=ps[pA][:, :], in1=sgc[i2][:, HALO:TW], op=ALU.mult),
                                     lambda e, pC=pC, i2=i2: e.tensor_tensor(out=hgc[i2][:, 0:HALO], in0=ps[pC][:, 0:HALO], in1=sgc[i2][:, 0:HALO], op=ALU.mult)],
                             reads=[bps[pA], bps[pC], bsgc[i2]], writes=[bhgc[i2]])
                        P.op("dve", lambda e, cb=cb, i2=i2: e.tensor_scalar(
                            out=y[:, cb, :], in0=hgc[i2][:, 2:2 + T], scalar1=PC[:, cb, 0:1], scalar2=PC[:, cb, 31:32],
                            op0=ALU.mult, op1=ALU.add), reads=[bhgc[i2], buf("PC")], writes=[by[cb]])
                        for j in range(1, KS):
                            P.op("dve", lambda e, cb=cb, i2=i2, j=j: e.scalar_tensor_tensor(
                                out=y[:, cb, :], in0=hgc[i2][:, 2 + j:2 + j + T], scalar=PC[:, cb, j:j + 1], in1=y[:, cb, :],
                                op0=ALU.mult, op1=ALU.add), reads=[bhgc[i2], by[cb]], writes=[by[cb]])
                        P.op("act", lambda e, cb=cb, i2=i2: e.activation(out=sqt[i2][:, :], in_=y[:, cb, :], func=AF.Square),
                             reads=[by[cb]], writes=[bsqt[i2]])
                        P.op("pe", lambda e, cb=cb: e.matmul(ps[6][:, :], lhsT=onesf[:, :], rhs=y[:, cb, :],
                                                            start=(cb == 0), stop=(cb == NCB - 1)),
                             reads=[buf("onesf"), by[cb]], writes=[bps[6]])
                        P.op("pe", lambda e, cb=cb, i2=i2: e.matmul(ps[7][:, :], lhsT=onesf[:, :], rhs=sqt[i2][:, :],
                                                                   start=(cb == 0), stop=(cb == NCB - 1)),
                             reads=[buf("onesf"), bsqt[i2]], writes=[bps[7]])
                cut("convmm")
                bmt, brs, bnm = buf("mt"), buf("rstdt"), buf("nmrt")
                P.op("dve", lambda e: e.tensor_scalar(out=mt[:, :], in0=ps[6][:, :], scalar1=1.0 / CW, scalar2=None, op0=ALU.mult),
                     reads=[bps[6]], writes=[bmt])
                P.op("dve", lambda e: e.tensor_tensor(out=nmrt[:, :], in0=mt[:, :], in1=mt[:, :], op=ALU.mult),
                     reads=[bmt], writes=[bnm])
                P.op("dve", lambda e: e.scalar_tensor_tensor(out=rstdt[:, :], in0=ps[7][:, :], scalar=1.0 / CW, in1=nmrt[:, :],
                                                             op0=ALU.mult, op1=ALU.subtract), reads=[bps[7], bnm], writes=[brs])
                P.op("dve", lambda e: e.tensor_scalar(out=rstdt[:, :], in0=rstdt[:, :], scalar1=EPS, scalar2=None, op0=ALU.add),
                     reads=[brs], writes=[brs])
                P.op("act", lambda e: e.activation(out=rstdt[:, :], in_=rstdt[:, :], func=AF.Sqrt), reads=[brs], writes=[brs])
                P.op("dve", lambda e: e.reciprocal(out=rstdt[:, :], in_=rstdt[:, :]), reads=[brs], writes=[brs])
                P.op("dve", lambda e: e.scalar_tensor_tensor(out=nmrt[:, :], in0=mt[:, :], scalar=-1.0, in1=rstdt[:, :],
                                                             op0=ALU.mult, op1=ALU.mult), reads=[bmt, brs], writes=[bnm])
                for cb in range(NCB):
                    P.op("dve", lambda e, cb=cb: e.tensor_tensor(out=y[:, cb, :], in0=y[:, cb, :], in1=rstdt[:, :], op=ALU.mult),
                         reads=[by[cb], brs], writes=[by[cb]])
                    P.op("dve", lambda e, cb=cb: e.tensor_tensor(out=y[:, cb, :], in0=y[:, cb, :], in1=nmrt[:, :], op=ALU.add),
                         reads=[by[cb], bnm], writes=[by[cb]])
                    P.op("act", lambda e, cb=cb: e.activation(out=a_out[:, cb, :], in_=y[:, cb, :], func=AF.Silu,
                                                              scale=PC[:, cb, 32:33], bias=PC[:, cb, 33:34]),
                         reads=[by[cb], buf("PC")], writes=[ba[cb]])

                cut("conv")
                bb = [buf(f"b{h}") for h in range(HEADS)]
                bol = [buf(f"ol{h}") for h in range(HEADS)]
                bgs = [buf(f"gs{h}") for h in range(HEADS)]
                hn = ["sgm", "fch", "pch", "pseg", "rp", "qs", "qt", "kt", "ktok", "vtok", "scm", "Sst0", "Sst1", "tmpS", "Sb", "vE", "vO"]
                hb_ = {n: buf("h_" + n) for n in hn}
                fence(ctemps, list(hb_.values()) + bol + bgs)
                P.op("dve", lambda e: e.memset(vE[64:128, :, :], 0.0), writes=[hb_["vE"]])
                P.op("dve", lambda e: e.memset(vO[0:64, :, :], 0.0), writes=[hb_["vO"]])
                fence([bring[5]], bb)
                def head_pass(h, so):
                    N = (lambda *a, **k: None) if so else P.op
                    s1_, s2_ = next_slot4(), next_slot4()
                    sl1 = ring[s1_][:, :].rearrange("p (dc two c) -> p dc two c", two=2, c=128)
                    sl2 = ring[s2_][:, :].rearrange("p (dc two c) -> p dc two c", two=2, c=128)
                    c_q, c_f, c_i, c_g = (2 * CW + h * 128, 3 * CW + h * 128, 4 * CW + h * 128, 5 * CW + h * 128)
                    P.op("pool", [lambda e, sl1=sl1, c0=c0, w=w: e.dma_start(
                        out=sl1[:, :, w, :], in_=w_in[:, c0:c0 + 128].rearrange("(dc p) c -> p dc c", p=128))
                        for w, c0 in (((1, c_f),) if so else ((0, c_q), (1, c_f)))], writes=[bring[s1_]], dma=f"r{s1_}")
                    P.op("pool", [lambda e, sl2=sl2, c0=c0, w=w: e.dma_start(
                        out=sl2[:, :, w, :], in_=w_in[:, c0:c0 + 128].rearrange("(dc p) c -> p dc c", p=128))
                        for w, c0 in (((0, c_i),) if so else ((0, c_i), (1, c_g)))], writes=[bring[s2_]], dma=f"r{s2_}")
                    for pi, sl, w, sidx in (((1, sl1, 1, s1_),) if so else ((0, sl1, 0, s1_), (1, sl1, 1, s1_), (2, sl2, 1, s2_))):
                        P.op("pe", [lambda e, dc=dc, sl=sl, w=w, pi=pi: e.matmul(
                            ps[pi][:, :], lhsT=sl[:, dc, w, :], rhs=hT[:, dc, HALO:TW],
                            start=(dc == 0), stop=(dc == DC - 1)) for dc in range(DC)],
                            reads=[bring[sidx], bhT], writes=[bps[pi]])
                    P.op("pe", [lambda e, dc=dc, tt=tt, sl2=sl2: e.matmul(
                        ps[3][:, tt * 128:(tt + 1) * 128], lhsT=hT[:, dc, HALO + tt * 128:HALO + (tt + 1) * 128],
                        rhs=sl2[:, dc, 0, :], start=(dc == 0), stop=(dc == DC - 1)) for tt in range(TT) for dc in range(DC)],
                        reads=[bring[s2_], bhT], writes=[bps[3]])
                    H = hb_
                    P.op("act", lambda e: e.activation(out=sgm[:, :], in_=ps[1][:, :], func=AF.Sigmoid, scale=-1.0),
                         reads=[bps[1]], writes=[H["sgm"]])
                    P.op("dve", lambda e, h=h: e.tensor_scalar(out=sgm[:, :], in0=sgm[:, :], scalar1=lbv[:, h, 1:2], scalar2=None,
                                                               op0=ALU.mult), reads=[H["sgm"], buf("lbv")], writes=[H["sgm"]])
                    P.op("dve", lambda e: e.tensor_scalar(out=fch[:, :], in0=sgm[:, :], scalar1=-1.0, scalar2=1.0,
                                                          op0=ALU.mult, op1=ALU.add), reads=[H["sgm"]], writes=[H["fch"]])
                    P.op("dve", lambda e: e.tensor_tensor_scan(out=pseg[:, :], data0=fch[:, :], data1=fch[:, :], initial=1.0,
                                                               op0=ALU.mult, op1=ALU.min), reads=[H["fch"]], writes=[H["pseg"]])
                    P.op("dve", [lambda e, j=j: e.tensor_tensor_scan(
                        out=pch[:, j * 64:(j + 1) * 64], data0=fch[:, j * 64:(j + 1) * 64], data1=fch[:, j * 64:(j + 1) * 64],
                        initial=1.0, op0=ALU.mult, op1=ALU.min) for j in range(8)], reads=[H["fch"]], writes=[H["pch"]])
                    N("act", lambda e: e.activation(out=qs[:, :], in_=ps[0][:, :], func=AF.Silu), reads=[bps[0]], writes=[H["qs"]])
                    N("dve", lambda e: e.tensor_tensor(out=qt[:, :], in0=qs[:, :], in1=pch[:, :], op=ALU.mult),
                         reads=[H["qs"], H["pch"]], writes=[H["qt"]])
                    N("dve", lambda e, h=h: e.tensor_tensor(out=b_out[:, h, :], in0=qs[:, :], in1=pseg[:, :], op=ALU.mult),
                         reads=[H["qs"], H["pseg"]], writes=[bb[h]])
                    P.op("dve", lambda e: e.tensor_scalar(out=rp[:, :], in0=pch[:, :], scalar1=1e-30, scalar2=None, op0=ALU.max),
                         reads=[H["pch"]], writes=[H["rp"]])
                    P.op("dve", lambda e: e.reciprocal(out=rp[:, :], in_=rp[:, :]), reads=[H["rp"]], writes=[H["rp"]])
                    P.op("dve", lambda e: e.tensor_tensor(out=kt[:, :], in0=sgm[:, :], in1=rp[:, :], op=ALU.mult),
                         reads=[H["sgm"], H["rp"]], writes=[H["kt"]])
                    N("act", lambda e, h=h: e.activation(out=gsT[:, h, :], in_=ps[2][:, :], func=AF.Silu),
                         reads=[bps[2]], writes=[bgs[h]])
                    N("act", lambda e: e.activation(out=vtok[:, :, :], in_=ps[3][:, :].rearrange("p (t v) -> p t v", v=128),
                                                       func=AF.Copy), reads=[bps[3]], writes=[H["vtok"]])
                    P.op("act", lambda e: e.activation(out=vE[0:64, :, :], in_=ps[3][0:64, :].rearrange("p (t v) -> p t v", v=128),
                                                       func=AF.Copy), reads=[bps[3]], writes=[H["vE"]])
                    P.op("act", lambda e: e.activation(out=vO[64:128, :, :], in_=ps[3][64:128, :].rearrange("p (t v) -> p t v", v=128),
                                                       func=AF.Copy), reads=[bps[3]], writes=[H["vO"]])
                    pst4 = ps[4].bitcast(BF16)
                    P.op("pe", [lambda e, tt=tt, pst4=pst4: e.transpose(out=pst4[:, tt * 128:(tt + 1) * 128],
                                                                        in_=kt[:, tt * 128:(tt + 1) * 128], identity=identb[:, :])
                                for tt in range(TT)], reads=[H["kt"], buf("identb")], writes=[bps[4]])
                    P.op("act", lambda e, pst4=pst4: e.activation(out=ktok[:, :], in_=pst4[:, 0:T], func=AF.Copy),
                         reads=[bps[4]], writes=[H["ktok"]])
                    N("pe", [lambda e, tt=tt: e.matmul(ps[5][:, tt * 128:(tt + 1) * 128], lhsT=kt[:, tt * 128:(tt + 1) * 128],
                                                          rhs=qt[:, tt * 128:(tt + 1) * 128], start=True, stop=True)
                                for tt in range(TT)], reads=[H["kt"], H["qt"]], writes=[bps[5]])
                    N("dve", lambda e: e.tensor_tensor(out=scm[:, :, :], in0=ps[5][:, :].rearrange("p (t v) -> p t v", v=128),
                                                          in1=maskb[:, :, :], op=ALU.mult),
                         reads=[bps[5], buf("maskb")], writes=[H["scm"]])
                    for rnd in range(2):
                        P.op("pe", [lambda e, j=j: e.matmul(
                            ps[7][:, (j % 4) * 128:(j % 4 + 1) * 128],
                            lhsT=ktok[:, (j // 2) * 128:(j // 2 + 1) * 128],
                            rhs=(vE if j % 2 == 0 else vO)[:, j // 2, :], start=True, stop=True)
                            for j in range(rnd * 4, rnd * 4 + 4)], reads=[H["ktok"], H["vE"], H["vO"]], writes=[bps[7]])
                        for j in range(rnd * 4, rnd * 4 + 4):
                            cur, nxt = Sst[j % 2], Sst[(j + 1) % 2]
                            bcur, bnxt = H[f"Sst{j % 2}"], H[f"Sst{(j + 1) % 2}"]
                            ej = pch[:, j * 64 + 63:j * 64 + 64]
                            uj = ps[7][:, (j % 4) * 128:(j % 4 + 1) * 128]
                            if j == 0:
                                P.op("dve", lambda e, nxt=nxt, ej=ej, uj=uj: e.tensor_scalar(
                                    out=nxt[:, :], in0=uj, scalar1=ej, scalar2=None, op0=ALU.mult),
                                    reads=[bps[7], H["pch"]], writes=[bnxt])
                            else:
                                P.op("dve", lambda e, cur=cur, uj=uj: e.tensor_tensor(out=tmpS[:, :], in0=uj, in1=cur[:, :], op=ALU.add),
                                     reads=[bps[7], bcur], writes=[H["tmpS"]])
                                P.op("dve", lambda e, nxt=nxt, ej=ej: e.tensor_scalar(
                                    out=nxt[:, :], in0=tmpS[:, :], scalar1=ej, scalar2=None, op0=ALU.mult),
                                    reads=[H["tmpS"], H["pch"]], writes=[bnxt])
                            if j < 7:
                                N("act", lambda e, nxt=nxt, j=j: e.activation(out=Sb[:, j + 1, :], in_=nxt[:, :], func=AF.Copy),
                                     reads=[bnxt], writes=[H["Sb"]])
                    for tt in range(TT):
                        fl = [lambda e, tt=tt: e.matmul(ps[6][:, tt * 128:(tt + 1) * 128], lhsT=vtok[:, tt, :], rhs=scm[:, tt, :],
                                                       start=True, stop=False)]
                        js = [j for j in (2 * tt, 2 * tt + 1) if j >= 1]
                        for j in js:
                            fl.append(lambda e, j=j, last=(j == js[-1]): e.matmul(
                                ps[6][:, j * 64:(j + 1) * 64], lhsT=Sb[:, j, :], rhs=qt[:, j * 64:(j + 1) * 64],
                                start=False, stop=last))
                        N("pe", fl, reads=[H["vtok"], H["scm"], H["Sb"], H["qt"]], writes=[bps[6]])
                    N("act", lambda e, h=h: e.activation(out=oloc[:, h, :], in_=ps[6][:, :], func=AF.Copy),
                         reads=[bps[6]], writes=[bol[h]])
                    if hf == 0:
                        g0r = (GR if so else 0) + h * 128
                        dv, bdv = (dvec2, buf("dvec2")) if so else (dvec, buf("dvec"))
                        P.op("sp", lambda e, g0r=g0r: e.dma_start(out=gin.ap()[g0r:g0r + 128, :], in_=Sst[0][:, :]),
                             reads=[H["Sst0"]], writes=[buf("gin")], dma="gi")
                        P.op("dve", lambda e, h=h, dv=dv: e.tensor_copy(out=dv[:, h:h + 1], in_=pseg[:, T - 1:T]),
                             reads=[H["pseg"]], writes=[bdv])

                for h in range(HEADS):
                    head_pass(h, False)
                if hf == 0 and nh == 2 and "gather" not in skip:
                    for t in range(TT):
                        r0 = TW + HALO + t * 128
                        P.op("sp", lambda e, r0=r0: e.dma_start(out=xstage[:, :], in_=xs[r0:r0 + 128, :]),
                             writes=[bring[0]], dma="xs0")
                        norm_transpose(xstage[:, :], 128, hT, HALO + t * 128, 0, [bring[0]], bhT, "p", hbt=(hb2, bring[1]))
                    for h in range(HEADS):
                        head_pass(h, True)

                cut("hgrn")
                bgin, bgout = buf("gin"), buf("gout")
                if hf == 0 and "gather" not in skip:
                    P.op("sp", [lambda e: e.dma_start(out=gin.ap()[2048:GR, :], in_=dvec[:, :]),
                                lambda e: e.dma_start(out=gin.ap()[GR + 2048:2 * GR, :], in_=dvec2[:, :])],
                         reads=[buf("dvec"), buf("dvec2")], writes=[bgin], dma="gi")
                    P.op("pool", lambda e: e.collective_compute(
                        "AllGather", ALU.bypass, replica_groups=[list(range(NCORES))],
                        ins=[gin.ap().opt()], outs=[gout.ap().opt()]),
                        reads=[bgin], writes=[bgout], cc="g0")
                bLr = [buf("Lr0"), buf("Lr1")]
                bSa, bda, bde, bSi = buf("Sacc"), [buf("dall0"), buf("dall1")], buf("deff"), buf("SinB")
                bof, bsq, brf = buf("of_"), buf("sqf"), buf("rsf")
                gtemps = bLr + [bSa, bde, bSi] + bda
                fence(list(hb_.values()), gtemps)
                Sacc2 = Sacc[:, :, :].rearrange("p h v -> p (h v)")
                P.op("dve", lambda e: e.memset(Sacc2, 0.0), writes=[bSa])
                step = 0
                hlist = list(range(hf + 1)) if "gather" not in skip else []
                g4 = gout.ap().rearrange("(r h k) c -> k r h c", r=NCORES, h=34, k=128)
                for hh in hlist:
                    P.op("sp", lambda e, hh=hh: e.dma_start(out=dall[hh][:, :, :], in_=g4[:, :, 17 * hh + 16, 0:HEADS]),
                         reads=[bgout], writes=[bda[hh]], dma=f"da{hh}")
                    mo = 0 if hh == hf else 16
                    for r in range(NCORES):
                        li = step % 2
                        step += 1
                        P.op("sp", lambda e, hh=hh, r=r, li=li: e.dma_start(out=Lr[li][:, :, :], in_=g4[:, r, 17 * hh:17 * hh + HEADS, :]),
                             reads=[bgout], writes=[bLr[li]], dma=f"L{li}")
                        P.op("dve", lambda e, hh=hh, r=r, mo=mo: e.tensor_scalar(
                            out=deff[:, :], in0=dall[hh][:, r, :], scalar1=selm[:, mo + r:mo + r + 1],
                            scalar2=selm[:, mo + 8 + r:mo + 9 + r], op0=ALU.mult, op1=ALU.add),
                            reads=[bda[hh], buf("selm")], writes=[bde])
                        P.op("dve", lambda e: e.tensor_tensor(
                            out=Sacc[:, :, :], in0=Sacc[:, :, :], in1=deff[:, :].unsqueeze(2).to_broadcast([128, HEADS, 128]),
                            op=ALU.mult), reads=[bde, bSa], writes=[bSa])
                        P.op("dve", lambda e, li=li, r=r, mo=mo: e.scalar_tensor_tensor(
                            out=Sacc2, in0=Lr[li][:, :, :].rearrange("p h v -> p (h v)"), scalar=selm[:, mo + r:mo + r + 1],
                            in1=Sacc2, op0=ALU.mult, op1=ALU.add), reads=[bLr[li], bSa, buf("selm")], writes=[bSa])
                P.op("act", lambda e: e.activation(out=SinB[:, :, :].rearrange("p h v -> p (h v)"), in_=Sacc2, func=AF.Copy),
                     reads=[bSa], writes=[bSi])

                fence(bLr, [bof, bsq, brf])
                for h in range(HEADS):
                    pq = 0 if h % 2 == 0 else 2
                    P.op("pe", lambda e, h=h, pq=pq: e.matmul(ps[pq + 1][:, :], lhsT=SinB[:, h, :], rhs=b_out[:, h, :],
                                                             start=True, stop=True),
                         reads=[bSi, bb[h]], writes=[bps[pq + 1]])
                    P.op("dve", lambda e, h=h, pq=pq: e.tensor_tensor(out=of_[:, :], in0=ps[pq + 1][:, :], in1=oloc[:, h, :], op=ALU.add),
                         reads=[bps[pq + 1], bol[h]], writes=[bof])
                    P.op("act", lambda e: e.activation(out=sqf[:, :], in_=of_[:, :], func=AF.Square), reads=[bof], writes=[bsq])
                    P.op("pe", lambda e, pq=pq: e.matmul(ps[pq][:, :], lhsT=onesf[:, :], rhs=sqf[:, :], start=True, stop=True),
                         reads=[buf("onesf"), bsq], writes=[bps[pq]])
                    P.op("dve", lambda e, pq=pq: e.tensor_scalar(out=rsf[:, :], in0=ps[pq][:, :], scalar1=1.0 / 128, scalar2=EPS,
                                                                 op0=ALU.mult, op1=ALU.add), reads=[bps[pq]], writes=[brf])
                    P.op("act", lambda e: e.activation(out=rsf[:, :], in_=rsf[:, :], func=AF.Sqrt), reads=[brf], writes=[brf])
                    P.op("dve", lambda e: e.reciprocal(out=rsf[:, :], in_=rsf[:, :]), reads=[brf], writes=[brf])
                    P.op("dve", lambda e: e.tensor_tensor(out=of_[:, :], in0=of_[:, :], in1=rsf[:, :], op=ALU.mult),
                         reads=[bof, brf], writes=[bof])
                    P.op("dve", lambda e, h=h: e.scalar_tensor_tensor(out=b_out[:, h, :], in0=of_[:, :], scalar=PC[:, h, 36:37],
                                                                      in1=gsT[:, h, :], op0=ALU.mult, op1=ALU.mult),
                         reads=[bof, bgs[h], buf("PC")], writes=[bb[h]])

                cut("fin")
                fence(ctemps + list(hb_.values()) + bol + bgs + [bof, bsq, brf] + gtemps, bx1)
                for t in range(TT):
                    r0 = hf * TW + HALO + t * 128
                    P.op("sp", lambda e, t=t, r0=r0: e.dma_start(out=x1[:, t, :], in_=xs[r0:r0 + 128, :]),
                         writes=[bx1[t]], dma=f"x{t}")
                for db in range(D // 512):
                    sA, sB = next_pair4()
                    wsl = [ring[sA][:, :].rearrange("p (mc c) -> p mc c", c=512), ring[sB][:, :].rearrange("p (mc c) -> p mc c", c=512)]
                    for q, sidx in ((0, sA), (1, sB)):
                        P.op("pool", lambda e, db=db, q=q, wsl=wsl: e.dma_start(
                            out=wsl[q], in_=w_out[q * 2048:(q + 1) * 2048, db * 512:(db + 1) * 512].rearrange("(mc p) c -> p mc c", p=128)),
                            writes=[bring[sidx]], dma=f"r{sidx}")
                    for t in range(TT):
                        pb = (db % 2) * 4 + t
                        P.op("pe", [lambda e, mc=mc, t=t, pb=pb, wsl=wsl: e.matmul(
                            ps[pb][:, :], lhsT=(a_out if mc < 16 else b_out)[:, mc % 16, t * 128:(t + 1) * 128],
                            rhs=wsl[mc // 16][:, mc % 16, :], start=(mc == 0), stop=(mc == 31)) for mc in range(32)],
                            reads=[bring[sA], bring[sB]] + ba + bb, writes=[bps[pb]])
                        P.op("dve", lambda e, t=t, db=db, pb=pb: e.tensor_tensor(
                            out=x1[:, t, db * 512:(db + 1) * 512], in0=ps[pb][:, :], in1=x1[:, t, db * 512:(db + 1) * 512], op=ALU.add),
                            reads=[bps[pb], bx1[t]], writes=[bx1[t]])
                fence(ba + bb, [bring[4], bring[5]])
                ring_i[0] = 0


            for t in range(TT):
                norm_transpose(x1[:, t, :], 128, h2T, t * 128, 32, [bx1[t]], buf("hT"), "f")
            for g in range(nfg if do_ffn else 0):
                sg_, su_, sd_ = next_slot(), next_slot(), next_slot()
                gsl = ring[sg_][:, :].rearrange("p (dc c) -> p dc c", c=256)
                usl = ring[su_][:, :].rearrange("p (dc c) -> p dc c", c=256)
                dsl = ring[sd_][:, :].rearrange("p (fc d) -> p fc d", d=D)
                P.op("pool", lambda e, g=g, gsl=gsl: e.dma_start(
                    out=gsl, in_=w_gate[:, g * 256:(g + 1) * 256].rearrange("(dc p) c -> p dc c", p=128)),
                    writes=[bring[sg_]], dma=f"r{sg_}")
                P.op("pool", lambda e, g=g, usl=usl: e.dma_start(
                    out=usl, in_=w_up[:, g * 256:(g + 1) * 256].rearrange("(dc p) c -> p dc c", p=128)),
                    writes=[bring[su_]], dma=f"r{su_}")
                P.op("pool", [lambda e, g=g, dsl=dsl, fc=fc, hh=hh: e.dma_start(
                    out=dsl[:, fc, hh * 2048:(hh + 1) * 2048],
                    in_=w_down[g * 256 + fc * 128:g * 256 + (fc + 1) * 128, hh * 2048:(hh + 1) * 2048])
                    for fc in range(2) for hh in range(2)],
                    writes=[bring[sd_]], dma=f"r{sd_}")
                ub = ubuf[g % 2]
                bu = buf(f"u{g % 2}")
                for fc in range(2):
                    pg, pu = 2 * fc, 2 * fc + 1
                    P.op("pe", [lambda e, dc=dc, fc=fc, pg=pg, gsl=gsl: e.matmul(
                        ps[pg][:, :], lhsT=gsl[:, dc, fc * 128:(fc + 1) * 128], rhs=h2T[:, dc, :],
                        start=(dc == 0), stop=(dc == DC - 1)) for dc in range(DC)],
                        reads=[bring[sg_], buf("hT")], writes=[bps[pg]])
                    P.op("pe", [lambda e, dc=dc, fc=fc, pu=pu, usl=usl: e.matmul(
                        ps[pu][:, :], lhsT=usl[:, dc, fc * 128:(fc + 1) * 128], rhs=h2T[:, dc, :],
                        start=(dc == 0), stop=(dc == DC - 1)) for dc in range(DC)],
                        reads=[bring[su_], buf("hT")], writes=[bps[pu]])
                    sgi = sgt[fc]
                    bsg = buf(f"sgt{fc}")
                    P.op("act", lambda e, pg=pg, sgi=sgi: e.activation(out=sgi[:, :], in_=ps[pg][:, :], func=AF.Silu),
                         reads=[bps[pg]], writes=[bsg])
                    P.op("dve", lambda e, pu=pu, sgi=sgi, fc=fc, ub=ub: e.tensor_tensor(
                        out=ub[:, fc, :], in0=ps[pu][:, :], in1=sgi[:, :], op=ALU.mult),
                        reads=[bps[pu], bsg], writes=[bu])
                k = 0
                for t in range(TT):
                    for db in range(D // 512):
                        pb = 4 + (k % 4)
                        k += 1
                        P.op("pe", [lambda e, fc=fc, t=t, db=db, pb=pb, ub=ub, dsl=dsl: e.matmul(
                            ps[pb][:, :], lhsT=ub[:, fc, t * 128:(t + 1) * 128], rhs=dsl[:, fc, db * 512:(db + 1) * 512],
                            start=(fc == 0), stop=(fc == 1)) for fc in range(2)],
                            reads=[bu, bring[sd_]], writes=[bps[pb]])
                        P.op("dve", lambda e, t=t, db=db, pb=pb: e.tensor_tensor(
                            out=x1[:, t, db * 512:(db + 1) * 512], in0=ps[pb][:, :],
                            in1=x1[:, t, db * 512:(db + 1) * 512], op=ALU.add),
                            reads=[bps[pb], bx1[t]], writes=[bx1[t]])

            st = buf("stat")
            allring = [bring[0], bring[1]]
            P.op("sp", lambda e: e.dma_start(out=gb[:, :], in_=fng[0:1, :].partition_broadcast(128)),
                 writes=[bring[0]], dma="gb")
            for t in range(TT):
                P.op("act", lambda e, t=t: e.activation(out=junk[:, :], in_=x1[:, t, :], func=AF.Square,
                                                       accum_out=stat[:, 0:1]),
                     reads=[bx1[t]], writes=[bring[1], st])
                P.op("dve", lambda e: e.tensor_scalar(out=stat[:, 1:2], in0=stat[:, 0:1], scalar1=1.0 / D,
                                                      scalar2=EPS, op0=ALU.mult, op1=ALU.add), reads=[st], writes=[st])
                P.op("act", lambda e: e.activation(out=stat[:, 2:3], in_=stat[:, 1:2], func=AF.Sqrt),
                     reads=[st], writes=[st])
                P.op("dve", lambda e: e.reciprocal(out=stat[:, 3:4], in_=stat[:, 2:3]), reads=[st], writes=[st])
                P.op("dve", lambda e, t=t: e.scalar_tensor_tensor(
                    out=x1[:, t, :], in0=x1[:, t, :], scalar=stat[:, 3:4], in1=gb[:, :], op0=ALU.mult, op1=ALU.mult),
                    reads=[bx1[t], st, bring[0]], writes=[bx1[t]])
                r0 = hf * T + t * 128
                P.op("sp", lambda e, t=t, r0=r0: e.dma_start(out=out[r0:r0 + 128, :], in_=x1[:, t, :]),
                     reads=[bx1[t]], dma=f"o{t}")
        for hf in range(nh):
            try:
                half(hf)
            except _Stop:
                break
        fin = []
        for t in range(TT):
            for k, v in buf(f"x1_{t}").r.items():
                fin.append((k, v))
        P.op("sp", lambda e: e.nop(), extra=fin)

        with nc.Block() as block:
            @block.tensor
            def _(e):
                P.replay("pe", e)

            @block.scalar
            def _(e):
                P.replay("act", e)

            @block.vector
            def _(e):
                P.replay("dve", e)

            @block.gpsimd
            def _(e):
                P.replay("pool", e)

            @block.sync
            def _(e):
                P.replay("sp", e)
    return nc


_NC_CACHE = {}


def _selm(c):
    m = (np.arange(NCORES) < c).astype(np.float32)
    row = np.concatenate([m, 1.0 - m, np.ones(NCORES, np.float32), np.zeros(NCORES, np.float32)])
    return np.ascontiguousarray(np.broadcast_to(row, (128, 32)).astype(np.float32))


def _host_layout(x, attn_norm_g, conv_w, conv_b, conv_ln_g, conv_ln_b, hg_lb_logits, hg_norm_g,
                 ffn_norm_g, final_norm_g):
    x2 = np.ascontiguousarray(x.reshape(SEQ, D))
    xpad = np.concatenate([np.zeros((HALO, D), np.float32), x2], axis=0)
    xs_list = []
    for c in range(NCORES):
        segs = [xpad[(c + 8 * h) * T:(c + 8 * h) * T + TW] for h in range(NH)]
        xs_list.append(np.ascontiguousarray(np.concatenate(segs, axis=0)))
    prm = np.zeros((NPR, CW), np.float32)
    prm[0:KS] = conv_w[0]
    prm[31] = conv_b[0]
    prm[32] = conv_ln_g[0]
    prm[33] = conv_ln_b[0]
    prm[34:36] = hg_lb_logits
    prm[36] = hg_norm_g[0]
    gns = np.concatenate([attn_norm_g[0].reshape(32, 128), ffn_norm_g[0].reshape(32, 128),
                          final_norm_g.reshape(32, 128)], axis=0).astype(np.float32)
    fng = np.ascontiguousarray(final_norm_g.reshape(1, D).astype(np.float32))
    return xs_list, prm, gns, fng


def kernel(x, attn_norm_g, w_in, conv_w, conv_b, conv_ln_g, conv_ln_b, hg_lb_logits,
           hg_norm_g, w_out, ffn_norm_g, w_gate, w_up, w_down, final_norm_g):
    f = lambda a: np.ascontiguousarray(np.asarray(a, dtype=np.float32))
    x = f(x)
    xs_list, prm, gns, fng = _host_layout(x, f(attn_norm_g), f(conv_w), f(conv_b), f(conv_ln_g), f(conv_ln_b),
                                          f(hg_lb_logits), f(hg_norm_g), f(ffn_norm_g), f(final_norm_g))
    cident = np.eye(128, dtype=np.float32)
    s = np.arange(128)
    cmask = ((s[:, None] <= s[None, :]) & ((s[:, None] // 64) == (s[None, :] // 64))).astype(np.float32)
    if "nc" not in _NC_CACHE:
        _NC_CACHE["nc"] = build_nc()
    nc = _NC_CACHE["nc"]
    shared = dict(w_in=f(w_in)[0], w_out=f(w_out)[0], w_gate=f(w_gate)[0], w_up=f(w_up)[0], w_down=f(w_down)[0],
                  prm=prm, gns=gns, fng=fng, cident=cident, cmask=cmask)
    in_maps = [dict(shared, xs=xs_list[c], selm=_selm(c)) for c in range(NCORES)]
    res = run_bass_kernel_spmd(nc, in_maps, core_ids=list(range(NCORES)))
    outp = np.zeros((SEQ, D), np.float32)
    for c in range(NCORES):
        o = np.asarray(res.results[c]["out"])
        for h in range(NH):
            sidx = c + 8 * h
            outp[sidx * T:(sidx + 1) * T] = o[h * T:(h + 1) * T]
    return outp.reshape(1, SEQ, D)
```

```python
import numpy as np
from contextlib import ExitStack
import concourse.bass as bass
import concourse.mybir as mybir
from concourse.bass_utils import run_bass_kernel_spmd

F32 = mybir.dt.float32
BF16 = mybir.dt.bfloat16
AF = mybir.ActivationFunctionType
ALU = mybir.AluOpType

D = 4096
DC = D // 128
SEQ = 8192
NCORES = 8
T = 512
TT = T // 128
NH = 2
HALO = 32
TW = HALO + T
CW = 2048
NCB = CW // 128
HEADS = 16
DFF = 11008
NFG = DFF // 256
INC = 12288
KS = 31
EPS = 1e-6
NPR = 40


class Buf:
    __slots__ = ("name", "w", "r")

    def __init__(self, name):
        self.name = name
        self.w = None
        self.r = {}


class Prog:
    ENG = ("pe", "act", "dve", "pool", "sp")
    LIMIT = 2000

    def __init__(self, nc, es):
        self.nc = nc
        self.es = es
        self.ops = {e: [] for e in self.ENG}
        self.cnt = {}
        self.sems = {}
        self.epoch = {}

    def _sem(self, key):
        if key not in self.sems:
            self.sems[key] = self.es.enter_context(self.nc.semaphore("s_" + key))
            self.cnt[key] = 0
        return self.sems[key]

    def op(self, eng, fns, reads=(), writes=(), dma=None, extra=(), cc=None):
        if callable(fns):
            fns = [fns]
        waits = {}

        def add(tok):
            if tok is None:
                return
            k, v = tok
            if waits.get(k, 0) < v:
                waits[k] = v

        for b in reads:
            add(b.w)
        for b in writes:
            add(b.w)
            for k, v in b.r.items():
                add((k, v))
        for t in extra:
            add(t)
        if cc is not None:
            base, n_inc, inc, allinc = "c_" + cc, 1, None, False
        elif dma is not None:
            base, n_inc, inc, allinc = "d_" + dma, 16 * len(fns), 16, True
        else:
            base, n_inc, inc, allinc = eng, 1, 1, False
        ep = self.epoch.get(base, 0)
        key = f"{base}_{ep}"
        self._sem(key)
        if self.cnt[key] + n_inc > self.LIMIT:
            ep += 1
            self.epoch[base] = ep
            key = f"{base}_{ep}"
            self._sem(key)
        self.cnt[key] += n_inc
        tok = (key, self.cnt[key])
        for b in reads:
            if b.r.get(key, 0) < tok[1]:
                b.r[key] = tok[1]
        for b in writes:
            b.w = tok
            b.r = {}
        self.ops[eng].append((waits, fns, key, inc, allinc))
        return tok

    def replay(self, eng, e):
        seen = {}
        for waits, fns, key, inc, allinc in self.ops[eng]:
            for k, v in waits.items():
                if seen.get(k, 0) >= v:
                    continue
                e.wait_ge(self.sems[k], v)
                seen[k] = v
            n = len(fns)
            for i, f in enumerate(fns):
                inst = f(e)
                if allinc or i == n - 1:
                    if inc is None:
                        inst.then_inc(self.sems[key])
                    else:
                        inst.then_inc(self.sems[key], inc)


def build_nc(do_attn=True, nh=NH, nfg=NFG, do_ffn=True, small_w=False, skip=(), stop_after=None):
    nc = bass.Bass("TRN2", target_bir_lowering=False)
    xs = nc.dram_tensor("xs", [NH * TW, D], F32, kind="ExternalInput").ap()
    w_in = nc.dram_tensor("w_in", [D, INC], F32, kind="ExternalInput").ap()
    w_out = nc.dram_tensor("w_out", [D, D], F32, kind="ExternalInput").ap()
    w_gate = nc.dram_tensor("w_gate", [128, 128] if small_w else [D, DFF], F32, kind="ExternalInput").ap()
    w_up = nc.dram_tensor("w_up", [128, 128] if small_w else [D, DFF], F32, kind="ExternalInput").ap()
    w_down = nc.dram_tensor("w_down", [128, 128] if small_w else [DFF, D], F32, kind="ExternalInput").ap()
    prm = nc.dram_tensor("prm", [NPR, CW], F32, kind="ExternalInput").ap()
    gns = nc.dram_tensor("gns", [96, 128], F32, kind="ExternalInput").ap()
    fng = nc.dram_tensor("fng", [1, D], F32, kind="ExternalInput").ap()
    cident = nc.dram_tensor("cident", [128, 128], F32, kind="ExternalInput").ap()
    cmask = nc.dram_tensor("cmask", [128, 128], F32, kind="ExternalInput").ap()
    selm_d = nc.dram_tensor("selm", [128, 32], F32, kind="ExternalInput").ap()
    GR = 17 * 128
    gin = nc.dram_tensor("gin", [2 * GR, 128], F32)
    gout = nc.dram_tensor("gout", [NCORES * 2 * GR, 128], F32)
    out = nc.dram_tensor("out", [NH * T, D], F32, kind="ExternalOutput").ap()

    es = ExitStack()
    with es:
        P = Prog(nc, es)
        off = [0]

        sbc = {}

        def sb(name, shape, dt, at=None):
            if name in sbc:
                return sbc[name]
            sbc[name] = _sb(name, shape, dt, at)
            return sbc[name]

        def _sb(name, shape, dt, at=None):
            if at is None:
                at = off[0]
                nb = int(np.prod(shape[1:])) * (4 if dt == F32 else 2)
                off[0] = at + ((nb + 31) // 32) * 32
            return nc.alloc_sbuf_tensor_at(name, shape, dt, offset=16512 + at)

        PC = sb("PC", [128, NCB, NPR], F32)
        GN = sb("GN", [128, 96], F32)
        identb = sb("identb", [128, 128], BF16)
        identf = sb("identf", [128, 128], F32)
        maskb = sb("maskb", [128, TT, 128], BF16)
        onesf = sb("onesf", [128, 128], F32)
        stat = sb("stat", [128, 16], F32)
        lbv = sb("lbv", [128, NCB, 2], F32)
        dvec = sb("dvec", [128, 128], F32)
        dvec2 = sb("dvec2", [128, 128], F32)
        selm = sb("selm_sb", [128, 32], F32)
        assert off[0] <= 8192, off[0]
        OFF_HT = 8192
        hT = sb("hT", [128, DC, TW], BF16, at=OFF_HT)
        OFF_SP = OFF_HT + DC * T * 2
        OFF_RING = OFF_HT + DC * TW * 2
        SLOT = 16384
        ring = [sb(f"ring{i}", [128, SLOT // 2], BF16, at=OFF_RING + i * SLOT) for i in range(4)]
        OFF_X1 = OFF_RING + 4 * SLOT
        x1 = sb("x1", [128, TT, D], F32, at=OFF_X1)
        OFF_AB = OFF_X1 + TT * D * 4
        ring += [sb(f"ring{i}", [128, SLOT // 2], BF16, at=OFF_AB + (i - 4) * SLOT) for i in (4, 5)]
        OFF_U = OFF_AB + 2 * SLOT
        ubuf = [sb(f"u{i}", [128, 2, T], BF16, at=OFF_U + i * 2048) for i in range(2)]
        assert 16512 + OFF_U + 4096 <= 229344
        h2T = sb("h2T", [128, DC, T], BF16, at=OFF_HT)
        sgt = [sb(f"sgt{i}", [128, T], BF16, at=OFF_SP + i * 1024) for i in range(2)]
        hb_main = sb("hb", [128, D], BF16, at=OFF_AB + SLOT)
        xstage = sb("xstage", [128, D], F32, at=OFF_RING)
        hb2 = sb("hb2", [128, D], BF16, at=OFF_RING + SLOT)
        gb = sb("gb", [128, D], F32, at=OFF_RING)
        junk = sb("junk", [128, D], BF16, at=OFF_RING + SLOT)

        ps = [es.enter_context(nc.psum_tensor(f"ps{i}", [128, 512], F32)) for i in range(8)]

        B = {}

        def buf(name):
            if name not in B:
                B[name] = Buf(name)
            return B[name]

        bps = [buf(f"ps{i}") for i in range(8)]
        bring = [buf(f"ring{i}") for i in range(6)]

        P.op("sp", lambda e: e.dma_start(out=identf[:, :], in_=cident[:, :]), writes=[buf("identf")], dma="c0")
        P.op("sp", lambda e: e.dma_start(out=onesf[:, :], in_=cmask[:, :]), writes=[buf("onesf")], dma="c1")
        P.op("dve", lambda e: e.tensor_copy(out=identb[:, :], in_=identf[:, :]), reads=[buf("identf")], writes=[buf("identb")])
        P.op("dve", [lambda e, t=t: e.tensor_copy(out=maskb[:, t, :], in_=onesf[:, :]) for t in range(TT)],
             reads=[buf("onesf")], writes=[buf("maskb")])
        P.op("dve", lambda e: e.memset(onesf[:, :], 1.0), reads=[buf("maskb")], writes=[buf("onesf")])
        P.op("sp", lambda e: e.dma_start(out=selm[:, :], in_=selm_d[:, :]), writes=[buf("selm")], dma="c3")
        P.op("dve", lambda e: e.memset(dvec[:, :], 0.0), writes=[buf("dvec")])
        P.op("dve", lambda e: e.memset(dvec2[:, :], 0.0), writes=[buf("dvec2")])
        gstage = sb("gstage", [96, 128], F32, at=OFF_X1)
        P.op("sp", lambda e: e.dma_start(out=gstage[:, :], in_=gns[:, :]), writes=[buf("x1_0")], dma="x0")
        P.op("pe", lambda e: e.transpose(out=ps[0][:, 0:96], in_=gstage[:, :], identity=identf[0:96, 0:96]),
             reads=[buf("x1_0"), buf("identf")], writes=[bps[0]])
        P.op("dve", lambda e: e.tensor_copy(out=GN[:, :], in_=ps[0][:, 0:96]), reads=[bps[0]], writes=[buf("GN")])

        def ffn_ring_state():
            return {"i": 0}

        def norm_transpose(src_tile_ap, rows, dstT, tok0, gcol, srcbufs, dstbuf, tag, hbt=None):
            st = buf("stat")
            hb, bhb = (hb_main, bring[5]) if hbt is None else hbt
            P.op("act", lambda e: e.activation(out=hb[0:rows, :], in_=src_tile_ap, func=AF.Square,
                                               accum_out=stat[0:rows, 0:1]),
                 reads=srcbufs, writes=[bhb, st])
            P.op("dve", lambda e: e.tensor_scalar(out=stat[0:rows, 1:2], in0=stat[0:rows, 0:1], scalar1=1.0 / D,
                                                  scalar2=EPS, op0=ALU.mult, op1=ALU.add),
                 reads=[st], writes=[st])
            P.op("act", lambda e: e.activation(out=stat[0:rows, 2:3], in_=stat[0:rows, 1:2], func=AF.Sqrt),
                 reads=[st], writes=[st])
            P.op("dve", lambda e: e.reciprocal(out=stat[0:rows, 3:4], in_=stat[0:rows, 2:3]), reads=[st], writes=[st])
            P.op("act", lambda e: e.activation(out=hb[0:rows, :], in_=src_tile_ap, func=AF.Copy,
                                               scale=stat[0:rows, 3:4]),
                 reads=srcbufs + [st], writes=[bhb])
            for g8 in range(DC // 8):
                pb = bps[4 + (g8 % 2)]
                pst = ps[4 + (g8 % 2)].bitcast(BF16)
                P.op("pe", [lambda e, j=j, g8=g8, pst=pst: e.transpose(
                    out=pst[:, j * 128:j * 128 + rows], in_=hb[0:rows, (g8 * 8 + j) * 128:(g8 * 8 + j + 1) * 128],
                    identity=identb[0:rows, 0:rows]) for j in range(8)],
                    reads=[bhb, buf("identb")], writes=[pb])
                src3 = pst[:, :].rearrange("p (j t) -> p j t", t=128)[:, :, 0:rows]
                g3 = GN[:, gcol + g8 * 8: gcol + g8 * 8 + 8].unsqueeze(2).to_broadcast([128, 8, rows])
                P.op("dve", lambda e, g8=g8, src3=src3, g3=g3: e.tensor_tensor(
                    out=dstT[:, g8 * 8:g8 * 8 + 8, tok0:tok0 + rows], in0=src3, in1=g3, op=ALU.mult),
                    reads=[pb, buf("GN")], writes=[dstbuf])

        ring_i = [0]

        def next_slot():
            i = ring_i[0]
            ring_i[0] = (i + 1) % 6
            return i

        ring4_i = [0]

        def next_slot4():
            i = ring4_i[0]
            ring4_i[0] = (i + 1) % 4
            return i

        def next_pair4():
            i = ring4_i[0]
            if i % 2 == 1:
                i = (i + 1) % 4
            ring4_i[0] = (i + 2) % 4
            return i, i + 1

        class _Stop(Exception):
            pass

        def cut(name):
            if stop_after == name:
                raise _Stop()

        def half(hf):
            bx1 = [buf(f"x1_{t}") for t in range(TT)]
            def load_x_tiles():
                for t in range(TT):
                    r0 = hf * TW + HALO + t * 128
                    P.op("sp", lambda e, t=t, r0=r0: e.dma_start(out=x1[:, t, :], in_=xs[r0:r0 + 128, :]),
                         writes=[bx1[t]], dma=f"x{t}")

            if not do_attn:
                load_x_tiles()

            if do_attn:
                def fence(src, dst):
                    for d_ in dst:
                        for s_ in src:
                            if s_.w is not None and d_.r.get(s_.w[0], 0) < s_.w[1]:
                                d_.r[s_.w[0]] = s_.w[1]
                            for k_, v_ in s_.r.items():
                                if d_.r.get(k_, 0) < v_:
                                    d_.r[k_] = v_

                XO = OFF_X1
                y = sb("y", [128, NCB, T], F32, at=XO)
                oloc = sb("oloc", [128, HEADS, T], BF16, at=XO)
                gsT = sb("gsT", [128, HEADS, T], BF16, at=XO + 16384)
                TO = XO + 32768
                sgc = [sb(f"sgc{i}", [128, TW], F32, at=TO + i * 2176) for i in range(2)]
                hgc = [sb(f"hgc{i}", [128, TW], F32, at=TO + 4352 + i * 2176) for i in range(2)]
                sqt = [sb(f"sqt{i}", [128, T], F32, at=TO + 8704 + i * 2048) for i in range(2)]
                mt = sb("mt", [128, T], F32, at=TO + 12800)
                rstdt = sb("rstdt", [128, T], F32, at=TO + 14848)
                nmrt = sb("nmrt", [128, T], F32, at=TO + 16896)
                y2 = [sb("y2_0", [128, T], F32, at=TO + 12800), sb("y2_1", [128, T], F32, at=TO + 14848)]
                sgm = sb("sgm", [128, T], F32, at=TO)
                fch = sb("fch", [128, T], F32, at=TO + 2048)
                pch = sb("pch", [128, T], F32, at=TO + 4096)
                pseg = sb("pseg", [128, T], F32, at=TO + 6144)
                rp = sb("rp", [128, T], F32, at=TO + 8192)
                qs = sb("qs", [128, T], F32, at=TO + 10240)
                qt = sb("qt", [128, T], BF16, at=TO + 12288)
                kt = sb("kt", [128, T], BF16, at=TO + 13312)
                ktok = sb("ktok", [128, T], BF16, at=TO + 14336)
                vtok = sb("vtok", [128, TT, 128], BF16, at=TO + 15360)
                scm = sb("scm", [128, TT, 128], BF16, at=TO + 16384)
                Sst = [sb(f"Sst{i}", [128, 128], F32, at=TO + 17408 + i * 512) for i in range(2)]
                tmpS = sb("tmpS", [128, 128], F32, at=TO + 18432)
                Sb = sb("Sb", [128, 8, 128], BF16, at=TO + 18944)
                vE = sb("vE", [128, TT, 128], BF16, at=TO + 20992)
                vO = sb("vO", [128, TT, 128], BF16, at=TO + 22016)
                of_ = sb("of_", [128, T], F32, at=TO)
                sqf = sb("sqf", [128, T], F32, at=TO + 2048)
                rsf = sb("rsf", [128, T], F32, at=TO + 4096)
                Lr = [sb(f"Lr{i}", [128, HEADS, 128], F32, at=TO + i * 8192) for i in range(2)]
                Sacc = sb("Sacc", [128, HEADS, 128], F32, at=TO + 16384)
                dall = [sb(f"dall{i}", [128, NCORES, HEADS], F32, at=TO + 24576 + i * 512) for i in range(2)]
                deff = sb("deff", [128, HEADS], F32, at=TO + 25600)
                SinB = sb("SinB", [128, HEADS, 128], BF16, at=TO + 25664)
                a_out = sb("a_out", [128, NCB, T], BF16, at=OFF_AB)
                b_out = sb("b_out", [128, HEADS, T], BF16, at=OFF_AB + SLOT)
                xh = sb("xh", [32, D], F32, at=OFF_AB)
                pstage = sb("pstage", [NPR, CW], F32, at=OFF_X1 + 16384)

                if hf == 0:
                    P.op("sp", lambda e: e.dma_start(out=pstage[:, :], in_=prm[:, :]), writes=[bx1[1]], dma="x1")
                    for half8 in range(2):
                        P.op("pe", [lambda e, cb=cb, half8=half8: e.transpose(
                            out=ps[1 + half8][:, (cb % 8) * NPR:(cb % 8 + 1) * NPR], in_=pstage[:, cb * 128:(cb + 1) * 128],
                            identity=identf[0:NPR, 0:NPR]) for cb in range(half8 * 8, half8 * 8 + 8)],
                            reads=[bx1[1], buf("identf")], writes=[bps[1 + half8]])
                        P.op("dve", lambda e, half8=half8: e.tensor_copy(
                            out=PC[:, half8 * 8:half8 * 8 + 8, :],
                            in_=ps[1 + half8][:, 0:8 * NPR].rearrange("p (c r) -> p c r", r=NPR)),
                            reads=[bps[1 + half8]], writes=[buf("PC")])
                    P.op("dve", lambda e: e.tensor_tensor(out=lbv[:, :, 0], in0=PC[:, :, 34], in1=PC[:, :, 35], op=ALU.subtract),
                         reads=[buf("PC")], writes=[buf("lbv")])
                    P.op("act", lambda e: e.activation(out=lbv[:, :, 0], in_=lbv[:, :, 0], func=AF.Sigmoid),
                         reads=[buf("lbv")], writes=[buf("lbv")])
                    P.op("dve", lambda e: e.tensor_scalar(out=lbv[:, :, 1], in0=lbv[:, :, 0], scalar1=-1.0, scalar2=1.0,
                                                          op0=ALU.mult, op1=ALU.add), reads=[buf("lbv")], writes=[buf("lbv")])

                cut("params")
                load_x_tiles()
                P.op("sp", lambda e, hf=hf: e.dma_start(out=xh[:, :], in_=xs[hf * TW:hf * TW + HALO, :]),
                     writes=[bring[4]], dma="xh")
                bhT = buf("hT")
                norm_transpose(xh[:, :], HALO, hT, 0, 0, [bring[4]], bhT, "h")
                for t in range(TT):
                    norm_transpose(x1[:, t, :], 128, hT, HALO + t * 128, 0, [bx1[t]], bhT, "a")

                cut("a0")
                by = [buf(f"y{cb}") for cb in range(NCB)]
                ba = [buf(f"a{cb}") for cb in range(NCB)]
                bsgc = [buf("sgc0"), buf("sgc1")]
                bhgc = [buf("hgc0"), buf("hgc1")]
                bsqt = [buf("sqt0"), buf("sqt1")]
                ctemps = by + bsgc + bhgc + bsqt + [buf("mt"), buf("rstdt"), buf("nmrt"), buf("y2_0"), buf("y2_1")]
                fence(bx1, ctemps)
                fence([bring[4]], ba)
                for cg in range(NCB // 2):
                    sv_, sg2_ = next_slot4(), next_slot4()
                    vsl = ring[sv_][:, :].rearrange("p (dc c) -> p dc c", c=256)
                    gsl2 = ring[sg2_][:, :].rearrange("p (dc c) -> p dc c", c=256)
                    P.op("pool", lambda e, cg=cg, vsl=vsl: e.dma_start(
                        out=vsl, in_=w_in[:, cg * 256:(cg + 1) * 256].rearrange("(dc p) c -> p dc c", p=128)),
                        writes=[bring[sv_]], dma=f"r{sv_}")
                    P.op("pool", lambda e, cg=cg, gsl2=gsl2: e.dma_start(
                        out=gsl2, in_=w_in[:, CW + cg * 256:CW + (cg + 1) * 256].rearrange("(dc p) c -> p dc c", p=128)),
                        writes=[bring[sg2_]], dma=f"r{sg2_}")
                    for cbl in range(2):
                        cb = 2 * cg + cbl
                        pA, pB, pC = (0, 1, 2) if cb % 2 == 0 else (3, 4, 5)
                        i2 = cb % 2
                        P.op("pe", [lambda e, dc=dc, cbl=cbl, pA=pA, vsl=vsl: e.matmul(
                            ps[pA][:, :], lhsT=vsl[:, dc, cbl * 128:(cbl + 1) * 128], rhs=hT[:, dc, HALO:TW],
                            start=(dc == 0), stop=(dc == DC - 1)) for dc in range(DC)],
                            reads=[bring[sv_], bhT], writes=[bps[pA]])
                        P.op("pe", [lambda e, dc=dc, cbl=cbl, pB=pB, gsl2=gsl2: e.matmul(
                            ps[pB][:, :], lhsT=gsl2[:, dc, cbl * 128:(cbl + 1) * 128], rhs=hT[:, dc, HALO:TW],
                            start=(dc == 0), stop=(dc == DC - 1)) for dc in range(DC)],
                            reads=[bring[sg2_], bhT], writes=[bps[pB]])
                        P.op("pe", [lambda e, dc=dc, cbl=cbl, pC=pC, vsl=vsl: e.matmul(
                            ps[pC][:, 0:HALO], lhsT=vsl[:, dc, cbl * 128:(cbl + 1) * 128], rhs=hT[:, dc, 0:HALO],
                            start=(dc == 0), stop=(dc == DC - 1)) for dc in range(DC)] +
                            [lambda e, dc=dc, cbl=cbl, pC=pC, gsl2=gsl2: e.matmul(
                            ps[pC][:, HALO:2 * HALO], lhsT=gsl2[:, dc, cbl * 128:(cbl + 1) * 128], rhs=hT[:, dc, 0:HALO],
                            start=(dc == 0), stop=(dc == DC - 1)) for dc in range(DC)],
                            reads=[bring[sv_], bring[sg2_], bhT], writes=[bps[pC]])
                        P.op("act", [lambda e, pB=pB, i2=i2: e.activation(out=sgc[i2][:, HALO:TW], in_=ps[pB][:, :], func=AF.Sigmoid),
                                     lambda e, pC=pC, i2=i2: e.activation(out=sgc[i2][:, 0:HALO], in_=ps[pC][:, HALO:2 * HALO], func=AF.Sigmoid)],
                             reads=[bps[pB], bps[pC]], writes=[bsgc[i2]])
                        P.op("dve", [lambda e, pA=pA, i2=i2: e.tensor_tensor(out=hgc[i2][:, HALO:TW], in0=ps[pA][:, :], in1=sgc[i2][:, HALO:TW], op=ALU.mult),
                                     lambda e, pC=pC, i2=i2: e.tensor_tensor(out=hgc[i2][:, 0:HALO], in0=ps[pC][:, 0:HALO], in1=sgc[i2][:, 0:HALO], op=ALU.mult)],
                             reads=[bps[pA], bps[pC], bsgc[i2]], writes=[bhgc[i2]])
                        P.op("dve", lambda e, cb=cb, i2=i2: e.tensor_scalar(
                            out=y[:, cb, :], in0=hgc[i2][:, 2:2 + T], scalar1=PC[:, cb, 0:1], scalar2=PC[:, cb, 31:32],
                            op0=ALU.mult, op1=ALU.add), reads=[bhgc[i2], buf("PC")], writes=[by[cb]])
                        by2 = buf(f"y2_{i2}")
                        P.op("dve", lambda e, cb=cb, i2=i2: e.tensor_scalar(
                            out=y2[i2][:, :], in0=hgc[i2][:, 3:3 + T], scalar1=PC[:, cb, 1:2], scalar2=None, op0=ALU.mult),
                            reads=[bhgc[i2], buf("PC")], writes=[by2])
                        for j in range(2, KS):
                            if j % 2 == 0:
                                P.op("dve", lambda e, cb=cb, i2=i2, j=j: e.scalar_tensor_tensor(
                                    out=y[:, cb, :], in0=hgc[i2][:, 2 + j:2 + j + T], scalar=PC[:, cb, j:j + 1], in1=y[:, cb, :],
                                    op0=ALU.mult, op1=ALU.add), reads=[bhgc[i2], by[cb]], writes=[by[cb]])
                            else:
                                P.op("dve", lambda e, cb=cb, i2=i2, j=j: e.scalar_tensor_tensor(
                                    out=y2[i2][:, :], in0=hgc[i2][:, 2 + j:2 + j + T], scalar=PC[:, cb, j:j + 1], in1=y2[i2][:, :],
                                    op0=ALU.mult, op1=ALU.add), reads=[bhgc[i2], by2], writes=[by2])
                        P.op("dve", lambda e, cb=cb, i2=i2: e.tensor_tensor(
                            out=y[:, cb, :], in0=y[:, cb, :], in1=y2[i2][:, :], op=ALU.add),
                            reads=[by[cb], by2], writes=[by[cb]])
                        P.op("act", lambda e, cb=cb, i2=i2: e.activation(out=sqt[i2][:, :], in_=y[:, cb, :], func=AF.Square),
                             reads=[by[cb]], writes=[bsqt[i2]])
                        P.op("pe", lambda e, cb=cb: e.matmul(ps[6][:, :], lhsT=onesf[:, :], rhs=y[:, cb, :],
                                                            start=(cb == 0), stop=(cb == NCB - 1)),
                             reads=[buf("onesf"), by[cb]], writes=[bps[6]])
                        P.op("pe", lambda e, cb=cb, i2=i2: e.matmul(ps[7][:, :], lhsT=onesf[:, :], rhs=sqt[i2][:, :],
                                                                   start=(cb == 0), stop=(cb == NCB - 1)),
                             reads=[buf("onesf"), bsqt[i2]], writes=[bps[7]])
                cut("convmm")
                bmt, brs, bnm = buf("mt"), buf("rstdt"), buf("nmrt")
                fence([buf("y2_0"), buf("y2_1")], [bmt, brs])
                P.op("dve", lambda e: e.tensor_scalar(out=mt[:, :], in0=ps[6][:, :], scalar1=1.0 / CW, scalar2=None, op0=ALU.mult),
                     reads=[bps[6]], writes=[bmt])
                P.op("dve", lambda e: e.tensor_tensor(out=nmrt[:, :], in0=mt[:, :], in1=mt[:, :], op=ALU.mult),
                     reads=[bmt], writes=[bnm])
                P.op("dve", lambda e: e.scalar_tensor_tensor(out=rstdt[:, :], in0=ps[7][:, :], scalar=1.0 / CW, in1=nmrt[:, :],
                                                             op0=ALU.mult, op1=ALU.subtract), reads=[bps[7], bnm], writes=[brs])
                P.op("dve", lambda e: e.tensor_scalar(out=rstdt[:, :], in0=rstdt[:, :], scalar1=EPS, scalar2=None, op0=ALU.add),
                     reads=[brs], writes=[brs])
                P.op("act", lambda e: e.activation(out=rstdt[:, :], in_=rstdt[:, :], func=AF.Sqrt), reads=[brs], writes=[brs])
                P.op("dve", lambda e: e.reciprocal(out=rstdt[:, :], in_=rstdt[:, :]), reads=[brs], writes=[brs])
                P.op("dve", lambda e: e.scalar_tensor_tensor(out=nmrt[:, :], in0=mt[:, :], scalar=-1.0, in1=rstdt[:, :],
                                                             op0=ALU.mult, op1=ALU.mult), reads=[bmt, brs], writes=[bnm])
                for cb in range(NCB):
                    P.op("dve", lambda e, cb=cb: e.tensor_tensor(out=y[:, cb, :], in0=y[:, cb, :], in1=rstdt[:, :], op=ALU.mult),
                         reads=[by[cb], brs], writes=[by[cb]])
                    P.op("dve", lambda e, cb=cb: e.tensor_tensor(out=y[:, cb, :], in0=y[:, cb, :], in1=nmrt[:, :], op=ALU.add),
                         reads=[by[cb], bnm], writes=[by[cb]])
                    P.op("act", lambda e, cb=cb: e.activation(out=a_out[:, cb, :], in_=y[:, cb, :], func=AF.Silu,
                                                              scale=PC[:, cb, 32:33], bias=PC[:, cb, 33:34]),
                         reads=[by[cb], buf("PC")], writes=[ba[cb]])

                cut("conv")
                bb = [buf(f"b{h}") for h in range(HEADS)]
                bol = [buf(f"ol{h}") for h in range(HEADS)]
                bgs = [buf(f"gs{h}") for h in range(HEADS)]
                hn = ["sgm", "fch", "pch", "pseg", "rp", "qs", "qt", "kt", "ktok", "vtok", "scm", "Sst0", "Sst1", "tmpS", "Sb", "vE", "vO"]
                hb_ = {n: buf("h_" + n) for n in hn}
                fence(ctemps, list(hb_.values()) + bol + bgs)
                P.op("dve", lambda e: e.memset(vE[64:128, :, :], 0.0), writes=[hb_["vE"]])
                P.op("dve", lambda e: e.memset(vO[0:64, :, :], 0.0), writes=[hb_["vO"]])
                fence([bring[5]], bb)
                def head_pass(h, so):
                    N = (lambda *a, **k: None) if so else P.op
                    s1_, s2_ = next_slot4(), next_slot4()
                    sl1 = ring[s1_][:, :].rearrange("p (dc two c) -> p dc two c", two=2, c=128)
                    sl2 = ring[s2_][:, :].rearrange("p (dc two c) -> p dc two c", two=2, c=128)
                    c_q, c_f, c_i, c_g = (2 * CW + h * 128, 3 * CW + h * 128, 4 * CW + h * 128, 5 * CW + h * 128)
                    P.op("pool", [lambda e, sl1=sl1, c0=c0, w=w: e.dma_start(
                        out=sl1[:, :, w, :], in_=w_in[:, c0:c0 + 128].rearrange("(dc p) c -> p dc c", p=128))
                        for w, c0 in (((1, c_f),) if so else ((0, c_q), (1, c_f)))], writes=[bring[s1_]], dma=f"r{s1_}")
                    P.op("pool", [lambda e, sl2=sl2, c0=c0, w=w: e.dma_start(
                        out=sl2[:, :, w, :], in_=w_in[:, c0:c0 + 128].rearrange("(dc p) c -> p dc c", p=128))
                        for w, c0 in (((0, c_i),) if so else ((0, c_i), (1, c_g)))], writes=[bring[s2_]], dma=f"r{s2_}")
                    for pi, sl, w, sidx in (((1, sl1, 1, s1_),) if so else ((0, sl1, 0, s1_), (1, sl1, 1, s1_), (2, sl2, 1, s2_))):
                        P.op("pe", [lambda e, dc=dc, sl=sl, w=w, pi=pi: e.matmul(
                            ps[pi][:, :], lhsT=sl[:, dc, w, :], rhs=hT[:, dc, HALO:TW],
                            start=(dc == 0), stop=(dc == DC - 1)) for dc in range(DC)],
                            reads=[bring[sidx], bhT], writes=[bps[pi]])
                    P.op("pe", [lambda e, dc=dc, tt=tt, sl2=sl2: e.matmul(
                        ps[3][:, tt * 128:(tt + 1) * 128], lhsT=hT[:, dc, HALO + tt * 128:HALO + (tt + 1) * 128],
                        rhs=sl2[:, dc, 0, :], start=(dc == 0), stop=(dc == DC - 1)) for tt in range(TT) for dc in range(DC)],
                        reads=[bring[s2_], bhT], writes=[bps[3]])
                    H = hb_
                    P.op("act", lambda e: e.activation(out=sgm[:, :], in_=ps[1][:, :], func=AF.Sigmoid, scale=-1.0),
                         reads=[bps[1]], writes=[H["sgm"]])
                    P.op("dve", lambda e, h=h: e.tensor_scalar(out=sgm[:, :], in0=sgm[:, :], scalar1=lbv[:, h, 1:2], scalar2=None,
                                                               op0=ALU.mult), reads=[H["sgm"], buf("lbv")], writes=[H["sgm"]])
                    P.op("dve", lambda e: e.tensor_scalar(out=fch[:, :], in0=sgm[:, :], scalar1=-1.0, scalar2=1.0,
                                                          op0=ALU.mult, op1=ALU.add), reads=[H["sgm"]], writes=[H["fch"]])
                    P.op("dve", lambda e: e.tensor_tensor_scan(out=pseg[:, :], data0=fch[:, :], data1=fch[:, :], initial=1.0,
                                                               op0=ALU.mult, op1=ALU.min), reads=[H["fch"]], writes=[H["pseg"]])
                    P.op("dve", [lambda e, j=j: e.tensor_tensor_scan(
                        out=pch[:, j * 64:(j + 1) * 64], data0=fch[:, j * 64:(j + 1) * 64], data1=fch[:, j * 64:(j + 1) * 64],
                        initial=1.0, op0=ALU.mult, op1=ALU.min) for j in range(8)], reads=[H["fch"]], writes=[H["pch"]])
                    N("act", lambda e: e.activation(out=qs[:, :], in_=ps[0][:, :], func=AF.Silu), reads=[bps[0]], writes=[H["qs"]])
                    N("dve", lambda e: e.tensor_tensor(out=qt[:, :], in0=qs[:, :], in1=pch[:, :], op=ALU.mult),
                         reads=[H["qs"], H["pch"]], writes=[H["qt"]])
                    N("dve", lambda e, h=h: e.tensor_tensor(out=b_out[:, h, :], in0=qs[:, :], in1=pseg[:, :], op=ALU.mult),
                         reads=[H["qs"], H["pseg"]], writes=[bb[h]])
                    P.op("dve", lambda e: e.tensor_scalar(out=rp[:, :], in0=pch[:, :], scalar1=1e-30, scalar2=None, op0=ALU.max),
                         reads=[H["pch"]], writes=[H["rp"]])
                    P.op("dve", lambda e: e.reciprocal(out=rp[:, :], in_=rp[:, :]), reads=[H["rp"]], writes=[H["rp"]])
                    P.op("dve", lambda e: e.tensor_tensor(out=kt[:, :], in0=sgm[:, :], in1=rp[:, :], op=ALU.mult),
                         reads=[H["sgm"], H["rp"]], writes=[H["kt"]])
                    N("act", lambda e, h=h: e.activation(out=gsT[:, h, :], in_=ps[2][:, :], func=AF.Silu),
                         reads=[bps[2]], writes=[bgs[h]])
                    N("act", lambda e: e.activation(out=vtok[:, :, :], in_=ps[3][:, :].rearrange("p (t v) -> p t v", v=128),
                                                       func=AF.Copy), reads=[bps[3]], writes=[H["vtok"]])
                    P.op("act", lambda e: e.activation(out=vE[0:64, :, :], in_=ps[3][0:64, :].rearrange("p (t v) -> p t v", v=128),
                                                       func=AF.Copy), reads=[bps[3]], writes=[H["vE"]])
                    P.op("act", lambda e: e.activation(out=vO[64:128, :, :], in_=ps[3][64:128, :].rearrange("p (t v) -> p t v", v=128),
                                                       func=AF.Copy), reads=[bps[3]], writes=[H["vO"]])
                    pst4 = ps[4].bitcast(BF16)
                    P.op("pe", [lambda e, tt=tt, pst4=pst4: e.transpose(out=pst4[:, tt * 128:(tt + 1) * 128],
                                                                        in_=kt[:, tt * 128:(tt + 1) * 128], identity=identb[:, :])
                                for tt in range(TT)], reads=[H["kt"], buf("identb")], writes=[bps[4]])
                    P.op("act", lambda e, pst4=pst4: e.activation(out=ktok[:, :], in_=pst4[:, 0:T], func=AF.Copy),
                         reads=[bps[4]], writes=[H["ktok"]])
                    N("pe", [lambda e, tt=tt: e.matmul(ps[5][:, tt * 128:(tt + 1) * 128], lhsT=kt[:, tt * 128:(tt + 1) * 128],
                                                          rhs=qt[:, tt * 128:(tt + 1) * 128], start=True, stop=True)
                                for tt in range(TT)], reads=[H["kt"], H["qt"]], writes=[bps[5]])
                    N("dve", lambda e: e.tensor_tensor(out=scm[:, :, :], in0=ps[5][:, :].rearrange("p (t v) -> p t v", v=128),
                                                          in1=maskb[:, :, :], op=ALU.mult),
                         reads=[bps[5], buf("maskb")], writes=[H["scm"]])
                    for rnd in range(2):
                        P.op("pe", [lambda e, j=j: e.matmul(
                            ps[7][:, (j % 4) * 128:(j % 4 + 1) * 128],
                            lhsT=ktok[:, (j // 2) * 128:(j // 2 + 1) * 128],
                            rhs=(vE if j % 2 == 0 else vO)[:, j // 2, :], start=True, stop=True)
                            for j in range(rnd * 4, rnd * 4 + 4)], reads=[H["ktok"], H["vE"], H["vO"]], writes=[bps[7]])
                        for j in range(rnd * 4, rnd * 4 + 4):
                            cur, nxt = Sst[j % 2], Sst[(j + 1) % 2]
                            bcur, bnxt = H[f"Sst{j % 2}"], H[f"Sst{(j + 1) % 2}"]
                            ej = pch[:, j * 64 + 63:j * 64 + 64]
                            uj = ps[7][:, (j % 4) * 128:(j % 4 + 1) * 128]
                            if j == 0:
                                P.op("dve", lambda e, nxt=nxt, ej=ej, uj=uj: e.tensor_scalar(
                                    out=nxt[:, :], in0=uj, scalar1=ej, scalar2=None, op0=ALU.mult),
                                    reads=[bps[7], H["pch"]], writes=[bnxt])
                            else:
                                P.op("dve", lambda e, cur=cur, uj=uj: e.tensor_tensor(out=tmpS[:, :], in0=uj, in1=cur[:, :], op=ALU.add),
                                     reads=[bps[7], bcur], writes=[H["tmpS"]])
                                P.op("dve", lambda e, nxt=nxt, ej=ej: e.tensor_scalar(
                                    out=nxt[:, :], in0=tmpS[:, :], scalar1=ej, scalar2=None, op0=ALU.mult),
                                    reads=[H["tmpS"], H["pch"]], writes=[bnxt])
                            if j < 7:
                                N("act", lambda e, nxt=nxt, j=j: e.activation(out=Sb[:, j + 1, :], in_=nxt[:, :], func=AF.Copy),
                                     reads=[bnxt], writes=[H["Sb"]])
                    for tt in range(TT):
                        fl = [lambda e, tt=tt: e.matmul(ps[6][:, tt * 128:(tt + 1) * 128], lhsT=vtok[:, tt, :], rhs=scm[:, tt, :],
                                                       start=True, stop=False)]
                        js = [j for j in (2 * tt, 2 * tt + 1) if j >= 1]
                        for j in js:
                            fl.append(lambda e, j=j, last=(j == js[-1]): e.matmul(
                                ps[6][:, j * 64:(j + 1) * 64], lhsT=Sb[:, j, :], rhs=qt[:, j * 64:(j + 1) * 64],
                                start=False, stop=last))
                        N("pe", fl, reads=[H["vtok"], H["scm"], H["Sb"], H["qt"]], writes=[bps[6]])
                    N("act", lambda e, h=h: e.activation(out=oloc[:, h, :], in_=ps[6][:, :], func=AF.Copy),
                         reads=[bps[6]], writes=[bol[h]])
                    if hf == 0:
                        g0r = (GR if so else 0) + h * 128
                        dv, bdv = (dvec2, buf("dvec2")) if so else (dvec, buf("dvec"))
                        P.op("sp", lambda e, g0r=g0r: e.dma_start(out=gin.ap()[g0r:g0r + 128, :], in_=Sst[0][:, :]),
                             reads=[H["Sst0"]], writes=[buf("gin")], dma="gi")
                        P.op("dve", lambda e, h=h, dv=dv: e.tensor_copy(out=dv[:, h:h + 1], in_=pseg[:, T - 1:T]),
                             reads=[H["pseg"]], writes=[bdv])

                for h in range(HEADS):
                    head_pass(h, False)
                if hf == 0 and nh == 2 and "gather" not in skip:
                    for t in range(TT):
                        r0 = TW + HALO + t * 128
                        P.op("sp", lambda e, r0=r0: e.dma_start(out=xstage[:, :], in_=xs[r0:r0 + 128, :]),
                             writes=[bring[0]], dma="xs0")
                        norm_transpose(xstage[:, :], 128, hT, HALO + t * 128, 0, [bring[0]], bhT, "p", hbt=(hb2, bring[1]))
                    for h in range(HEADS):
                        head_pass(h, True)

                cut("hgrn")
                bgin, bgout = buf("gin"), buf("gout")
                if hf == 0 and "gather" not in skip:
                    P.op("sp", [lambda e: e.dma_start(out=gin.ap()[2048:GR, :], in_=dvec[:, :]),
                                lambda e: e.dma_start(out=gin.ap()[GR + 2048:2 * GR, :], in_=dvec2[:, :])],
                         reads=[buf("dvec"), buf("dvec2")], writes=[bgin], dma="gi")
                    P.op("pool", lambda e: e.collective_compute(
                        "AllGather", ALU.bypass, replica_groups=[list(range(NCORES))],
                        ins=[gin.ap().opt()], outs=[gout.ap().opt()]),
                        reads=[bgin], writes=[bgout], cc="g0")
                bLr = [buf("Lr0"), buf("Lr1")]
                bSa, bda, bde, bSi = buf("Sacc"), [buf("dall0"), buf("dall1")], buf("deff"), buf("SinB")
                bof, bsq, brf = buf("of_"), buf("sqf"), buf("rsf")
                gtemps = bLr + [bSa, bde, bSi] + bda
                fence(list(hb_.values()), gtemps)
                Sacc2 = Sacc[:, :, :].rearrange("p h v -> p (h v)")
                P.op("dve", lambda e: e.memset(Sacc2, 0.0), writes=[bSa])
                step = 0
                hlist = list(range(hf + 1)) if "gather" not in skip else []
                g4 = gout.ap().rearrange("(r h k) c -> k r h c", r=NCORES, h=34, k=128)
                for hh in hlist:
                    P.op("sp", lambda e, hh=hh: e.dma_start(out=dall[hh][:, :, :], in_=g4[:, :, 17 * hh + 16, 0:HEADS]),
                         reads=[bgout], writes=[bda[hh]], dma=f"da{hh}")
                    mo = 0 if hh == hf else 16
                    for r in range(NCORES):
                        li = step % 2
                        step += 1
                        P.op("sp", lambda e, hh=hh, r=r, li=li: e.dma_start(out=Lr[li][:, :, :], in_=g4[:, r, 17 * hh:17 * hh + HEADS, :]),
                             reads=[bgout], writes=[bLr[li]], dma=f"L{li}")
                        P.op("dve", lambda e, hh=hh, r=r, mo=mo: e.tensor_scalar(
                            out=deff[:, :], in0=dall[hh][:, r, :], scalar1=selm[:, mo + r:mo + r + 1],
                            scalar2=selm[:, mo + 8 + r:mo + 9 + r], op0=ALU.mult, op1=ALU.add),
                            reads=[bda[hh], buf("selm")], writes=[bde])
                        P.op("dve", lambda e: e.tensor_tensor(
                            out=Sacc[:, :, :], in0=Sacc[:, :, :], in1=deff[:, :].unsqueeze(2).to_broadcast([128, HEADS, 128]),
                            op=ALU.mult), reads=[bde, bSa], writes=[bSa])
                        P.op("dve", lambda e, li=li, r=r, mo=mo: e.scalar_tensor_tensor(
                            out=Sacc2, in0=Lr[li][:, :, :].rearrange("p h v -> p (h v)"), scalar=selm[:, mo + r:mo + r + 1],
                            in1=Sacc2, op0=ALU.mult, op1=ALU.add), reads=[bLr[li], bSa, buf("selm")], writes=[bSa])
                P.op("act", lambda e: e.activation(out=SinB[:, :, :].rearrange("p h v -> p (h v)"), in_=Sacc2, func=AF.Copy),
                     reads=[bSa], writes=[bSi])

                fence(bLr, [bof, bsq, brf])
                for h in range(HEADS):
                    pq = 0 if h % 2 == 0 else 2
                    P.op("pe", lambda e, h=h, pq=pq: e.matmul(ps[pq + 1][:, :], lhsT=SinB[:, h, :], rhs=b_out[:, h, :],
                                                             start=True, stop=True),
                         reads=[bSi, bb[h]], writes=[bps[pq + 1]])
                    P.op("dve", lambda e, h=h, pq=pq: e.tensor_tensor(out=of_[:, :], in0=ps[pq + 1][:, :], in1=oloc[:, h, :], op=ALU.add),
                         reads=[bps[pq + 1], bol[h]], writes=[bof])
                    P.op("act", lambda e: e.activation(out=sqf[:, :], in_=of_[:, :], func=AF.Square), reads=[bof], writes=[bsq])
                    P.op("pe", lambda e, pq=pq: e.matmul(ps[pq][:, :], lhsT=onesf[:, :], rhs=sqf[:, :], start=True, stop=True),
                         reads=[buf("onesf"), bsq], writes=[bps[pq]])
                    P.op("dve", lambda e, pq=pq: e.tensor_scalar(out=rsf[:, :], in0=ps[pq][:, :], scalar1=1.0 / 128, scalar2=EPS,
                                                                 op0=ALU.mult, op1=ALU.add), reads=[bps[pq]], writes=[brf])
                    P.op("act", lambda e: e.activation(out=rsf[:, :], in_=rsf[:, :], func=AF.Sqrt), reads=[brf], writes=[brf])
                    P.op("dve", lambda e: e.reciprocal(out=rsf[:, :], in_=rsf[:, :]), reads=[brf], writes=[brf])
                    P.op("dve", lambda e: e.tensor_tensor(out=of_[:, :], in0=of_[:, :], in1=rsf[:, :], op=ALU.mult),
                         reads=[bof, brf], writes=[bof])
                    P.op("dve", lambda e, h=h: e.scalar_tensor_tensor(out=b_out[:, h, :], in0=of_[:, :], scalar=PC[:, h, 36:37],
                                                                      in1=gsT[:, h, :], op0=ALU.mult, op1=ALU.mult),
                         reads=[bof, bgs[h], buf("PC")], writes=[bb[h]])

                cut("fin")
                fence(ctemps + list(hb_.values()) + bol + bgs + [bof, bsq, brf] + gtemps, bx1)
                for t in range(TT):
                    r0 = hf * TW + HALO + t * 128
                    P.op("sp", lambda e, t=t, r0=r0: e.dma_start(out=x1[:, t, :], in_=xs[r0:r0 + 128, :]),
                         writes=[bx1[t]], dma=f"x{t}")
                for db in range(D // 512):
                    sA, sB = next_pair4()
                    wsl = [ring[sA][:, :].rearrange("p (mc c) -> p mc c", c=512), ring[sB][:, :].rearrange("p (mc c) -> p mc c", c=512)]
                    for q, sidx in ((0, sA), (1, sB)):
                        P.op("pool", lambda e, db=db, q=q, wsl=wsl: e.dma_start(
                            out=wsl[q], in_=w_out[q * 2048:(q + 1) * 2048, db * 512:(db + 1) * 512].rearrange("(mc p) c -> p mc c", p=128)),
                            writes=[bring[sidx]], dma=f"r{sidx}")
                    for t in range(TT):
                        pb = (db % 2) * 4 + t
                        P.op("pe", [lambda e, mc=mc, t=t, pb=pb, wsl=wsl: e.matmul(
                            ps[pb][:, :], lhsT=(a_out if mc < 16 else b_out)[:, mc % 16, t * 128:(t + 1) * 128],
                            rhs=wsl[mc // 16][:, mc % 16, :], start=(mc == 0), stop=(mc == 31)) for mc in range(32)],
                            reads=[bring[sA], bring[sB]] + ba + bb, writes=[bps[pb]])
                        P.op("dve", lambda e, t=t, db=db, pb=pb: e.tensor_tensor(
                            out=x1[:, t, db * 512:(db + 1) * 512], in0=ps[pb][:, :], in1=x1[:, t, db * 512:(db + 1) * 512], op=ALU.add),
                            reads=[bps[pb], bx1[t]], writes=[bx1[t]])
                fence(ba + bb, [bring[4], bring[5]])
                ring_i[0] = 0


            for t in range(TT):
                norm_transpose(x1[:, t, :], 128, h2T, t * 128, 32, [bx1[t]], buf("hT"), "f")
            for g in range(nfg if do_ffn else 0):
                sg_, su_, sd_ = next_slot(), next_slot(), next_slot()
                gsl = ring[sg_][:, :].rearrange("p (dc c) -> p dc c", c=256)
                usl = ring[su_][:, :].rearrange("p (dc c) -> p dc c", c=256)
                dsl = ring[sd_][:, :].rearrange("p (fc d) -> p fc d", d=D)
                P.op("pool", lambda e, g=g, gsl=gsl: e.dma_start(
                    out=gsl, in_=w_gate[:, g * 256:(g + 1) * 256].rearrange("(dc p) c -> p dc c", p=128)),
                    writes=[bring[sg_]], dma=f"r{sg_}")
                P.op("pool", lambda e, g=g, usl=usl: e.dma_start(
                    out=usl, in_=w_up[:, g * 256:(g + 1) * 256].rearrange("(dc p) c -> p dc c", p=128)),
                    writes=[bring[su_]], dma=f"r{su_}")
                P.op("pool", [lambda e, g=g, dsl=dsl, fc=fc, hh=hh: e.dma_start(
                    out=dsl[:, fc, hh * 2048:(hh + 1) * 2048],
                    in_=w_down[g * 256 + fc * 128:g * 256 + (fc + 1) * 128, hh * 2048:(hh + 1) * 2048])
                    for fc in range(2) for hh in range(2)],
                    writes=[bring[sd_]], dma=f"r{sd_}")
                ub = ubuf[g % 2]
                bu = buf(f"u{g % 2}")
                for fc in range(2):
                    pg, pu = 2 * fc, 2 * fc + 1
                    P.op("pe", [lambda e, dc=dc, fc=fc, pg=pg, gsl=gsl: e.matmul(
                        ps[pg][:, :], lhsT=gsl[:, dc, fc * 128:(fc + 1) * 128], rhs=h2T[:, dc, :],
                        start=(dc == 0), stop=(dc == DC - 1)) for dc in range(DC)],
                        reads=[bring[sg_], buf("hT")], writes=[bps[pg]])
                    P.op("pe", [lambda e, dc=dc, fc=fc, pu=pu, usl=usl: e.matmul(
                        ps[pu][:, :], lhsT=usl[:, dc, fc * 128:(fc + 1) * 128], rhs=h2T[:, dc, :],
                        start=(dc == 0), stop=(dc == DC - 1)) for dc in range(DC)],
                        reads=[bring[su_], buf("hT")], writes=[bps[pu]])
                    sgi = sgt[fc]
                    bsg = buf(f"sgt{fc}")
                    P.op("act", lambda e, pg=pg, sgi=sgi: e.activation(out=sgi[:, :], in_=ps[pg][:, :], func=AF.Silu),
                         reads=[bps[pg]], writes=[bsg])
                    P.op("dve", lambda e, pu=pu, sgi=sgi, fc=fc, ub=ub: e.tensor_tensor(
                        out=ub[:, fc, :], in0=ps[pu][:, :], in1=sgi[:, :], op=ALU.mult),
                        reads=[bps[pu], bsg], writes=[bu])
                k = 0
                for t in range(TT):
                    for db in range(D // 512):
                        pb = 4 + (k % 4)
                        k += 1
                        P.op("pe", [lambda e, fc=fc, t=t, db=db, pb=pb, ub=ub, dsl=dsl: e.matmul(
                            ps[pb][:, :], lhsT=ub[:, fc, t * 128:(t + 1) * 128], rhs=dsl[:, fc, db * 512:(db + 1) * 512],
                            start=(fc == 0), stop=(fc == 1)) for fc in range(2)],
                            reads=[bu, bring[sd_]], writes=[bps[pb]])
                        P.op("dve", lambda e, t=t, db=db, pb=pb: e.tensor_tensor(
                            out=x1[:, t, db * 512:(db + 1) * 512], in0=ps[pb][:, :],
                            in1=x1[:, t, db * 512:(db + 1) * 512], op=ALU.add),
                            reads=[bps[pb], bx1[t]], writes=[bx1[t]])

            st = buf("stat")
            allring = [bring[0], bring[1]]
            P.op("sp", lambda e: e.dma_start(out=gb[:, :], in_=fng[0:1, :].partition_broadcast(128)),
                 writes=[bring[0]], dma="gb")
            for t in range(TT):
                P.op("act", lambda e, t=t: e.activation(out=junk[:, :], in_=x1[:, t, :], func=AF.Square,
                                                       accum_out=stat[:, 0:1]),
                     reads=[bx1[t]], writes=[bring[1], st])
                P.op("dve", lambda e: e.tensor_scalar(out=stat[:, 1:2], in0=stat[:, 0:1], scalar1=1.0 / D,
                                                      scalar2=EPS, op0=ALU.mult, op1=ALU.add), reads=[st], writes=[st])
                P.op("act", lambda e: e.activation(out=stat[:, 2:3], in_=stat[:, 1:2], func=AF.Sqrt),
                     reads=[st], writes=[st])
                P.op("dve", lambda e: e.reciprocal(out=stat[:, 3:4], in_=stat[:, 2:3]), reads=[st], writes=[st])
                P.op("dve", lambda e, t=t: e.scalar_tensor_tensor(
                    out=x1[:, t, :], in0=x1[:, t, :], scalar=stat[:, 3:4], in1=gb[:, :], op0=ALU.mult, op1=ALU.mult),
                    reads=[bx1[t], st, bring[0]], writes=[bx1[t]])
                r0 = hf * T + t * 128
                P.op("sp", lambda e, t=t, r0=r0: e.dma_start(out=out[r0:r0 + 128, :], in_=x1[:, t, :]),
                     reads=[bx1[t]], dma=f"o{t}")
        for hf in range(nh):
            try:
                half(hf)
            except _Stop:
                break
        fin = []
        for t in range(TT):
            for k, v in buf(f"x1_{t}").r.items():
                fin.append((k, v))
        P.op("sp", lambda e: e.nop(), extra=fin)

        with nc.Block() as block:
            @block.tensor
            def _(e):
                P.replay("pe", e)

            @block.scalar
            def _(e):
                P.replay("act", e)

            @block.vector
            def _(e):
                P.replay("dve", e)

            @block.gpsimd
            def _(e):
                P.replay("pool", e)

            @block.sync
            def _(e):
                P.replay("sp", e)
    return nc


_NC_CACHE = {}


def _selm(c):
    m = (np.arange(NCORES) < c).astype(np.float32)
    row = np.concatenate([m, 1.0 - m, np.ones(NCORES, np.float32), np.zeros(NCORES, np.float32)])
    return np.ascontiguousarray(np.broadcast_to(row, (128, 32)).astype(np.float32))


def _host_layout(x, attn_norm_g, conv_w, conv_b, conv_ln_g, conv_ln_b, hg_lb_logits, hg_norm_g,
                 ffn_norm_g, final_norm_g):
    x2 = np.ascontiguousarray(x.reshape(SEQ, D))
    xpad = np.concatenate([np.zeros((HALO, D), np.float32), x2], axis=0)
    xs_list = []
    for c in range(NCORES):
        segs = [xpad[(c + 8 * h) * T:(c + 8 * h) * T + TW] for h in range(NH)]
        xs_list.append(np.ascontiguousarray(np.concatenate(segs, axis=0)))
    prm = np.zeros((NPR, CW), np.float32)
    prm[0:KS] = conv_w[0]
    prm[31] = conv_b[0]
    prm[32] = conv_ln_g[0]
    prm[33] = conv_ln_b[0]
    prm[34:36] = hg_lb_logits
    prm[36] = hg_norm_g[0]
    gns = np.concatenate([attn_norm_g[0].reshape(32, 128), ffn_norm_g[0].reshape(32, 128),
                          final_norm_g.reshape(32, 128)], axis=0).astype(np.float32)
    fng = np.ascontiguousarray(final_norm_g.reshape(1, D).astype(np.float32))
    return xs_list, prm, gns, fng


def kernel(x, attn_norm_g, w_in, conv_w, conv_b, conv_ln_g, conv_ln_b, hg_lb_logits,
           hg_norm_g, w_out, ffn_norm_g, w_gate, w_up, w_down, final_norm_g):
    f = lambda a: np.ascontiguousarray(np.asarray(a, dtype=np.float32))
    x = f(x)
    xs_list, prm, gns, fng = _host_layout(x, f(attn_norm_g), f(conv_w), f(conv_b), f(conv_ln_g), f(conv_ln_b),
                                          f(hg_lb_logits), f(hg_norm_g), f(ffn_norm_g), f(final_norm_g))
    cident = np.eye(128, dtype=np.float32)
    s = np.arange(128)
    cmask = ((s[:, None] <= s[None, :]) & ((s[:, None] // 64) == (s[None, :] // 64))).astype(np.float32)
    if "nc" not in _NC_CACHE:
        _NC_CACHE["nc"] = build_nc()
    nc = _NC_CACHE["nc"]
    shared = dict(w_in=f(w_in)[0], w_out=f(w_out)[0], w_gate=f(w_gate)[0], w_up=f(w_up)[0], w_down=f(w_down)[0],
                  prm=prm, gns=gns, fng=fng, cident=cident, cmask=cmask)
    in_maps = [dict(shared, xs=xs_list[c], selm=_selm(c)) for c in range(NCORES)]
    res = run_bass_kernel_spmd(nc, in_maps, core_ids=list(range(NCORES)))
    outp = np.zeros((SEQ, D), np.float32)
    for c in range(NCORES):
        o = np.asarray(res.results[c]["out"])
        for h in range(NH):
            sidx = c + 8 * h
            outp[sidx * T:(sidx + 1) * T] = o[h * T:(h + 1) * T]
    return outp.reshape(1, SEQ, D)
```

```python
import numpy as np
from contextlib import ExitStack
import concourse.bass as bass
import concourse.mybir as mybir
from concourse.bass_utils import run_bass_kernel_spmd

F32 = mybir.dt.float32
BF16 = mybir.dt.bfloat16
AF = mybir.ActivationFunctionType
ALU = mybir.AluOpType

D = 4096
DC = D // 128
SEQ = 8192
NCORES = 8
T = 512
TT = T // 128
NH = 2
HALO = 32
TW = HALO + T
CW = 2048
NCB = CW // 128
HEADS = 16
DFF = 11008
NFG = DFF // 256
INC = 12288
KS = 31
EPS = 1e-6
NPR = 40


class Buf:
    __slots__ = ("name", "w", "r")

    def __init__(self, name):
        self.name = name
        self.w = None
        self.r = {}


class Prog:
    ENG = ("pe", "act", "dve", "pool", "sp")
    LIMIT = 2000

    def __init__(self, nc, es):
        self.nc = nc
        self.es = es
        self.ops = {e: [] for e in self.ENG}
        self.cnt = {}
        self.sems = {}
        self.epoch = {}

    def _sem(self, key):
        if key not in self.sems:
            self.sems[key] = self.es.enter_context(self.nc.semaphore("s_" + key))
            self.cnt[key] = 0
        return self.sems[key]

    def op(self, eng, fns, reads=(), writes=(), dma=None, extra=(), cc=None):
        if callable(fns):
            fns = [fns]
        waits = {}

        def add(tok):
            if tok is None:
                return
            k, v = tok
            if waits.get(k, 0) < v:
                waits[k] = v

        for b in reads:
            add(b.w)
        for b in writes:
            add(b.w)
            for k, v in b.r.items():
                add((k, v))
        for t in extra:
            add(t)
        if cc is not None:
            base, n_inc, inc, allinc = "c_" + cc, 1, None, False
        elif dma is not None:
            base, n_inc, inc, allinc = "d_" + dma, 16 * len(fns), 16, True
        else:
            base, n_inc, inc, allinc = eng, 1, 1, False
        ep = self.epoch.get(base, 0)
        key = f"{base}_{ep}"
        self._sem(key)
        if self.cnt[key] + n_inc > self.LIMIT:
            ep += 1
            self.epoch[base] = ep
            key = f"{base}_{ep}"
            self._sem(key)
        self.cnt[key] += n_inc
        tok = (key, self.cnt[key])
        for b in reads:
            if b.r.get(key, 0) < tok[1]:
                b.r[key] = tok[1]
        for b in writes:
            b.w = tok
            b.r = {}
        self.ops[eng].append((waits, fns, key, inc, allinc))
        return tok

    def replay(self, eng, e):
        seen = {}
        for waits, fns, key, inc, allinc in self.ops[eng]:
            for k, v in waits.items():
                if seen.get(k, 0) >= v:
                    continue
                e.wait_ge(self.sems[k], v)
                seen[k] = v
            n = len(fns)
            for i, f in enumerate(fns):
                inst = f(e)
                if allinc or i == n - 1:
                    if inc is None:
                        inst.then_inc(self.sems[key])
                    else:
                        inst.then_inc(self.sems[key], inc)


def build_nc(do_attn=True, nh=NH, nfg=NFG, do_ffn=True, small_w=False, skip=(), stop_after=None):
    nc = bass.Bass("TRN2", target_bir_lowering=False)
    xs = nc.dram_tensor("xs", [NH * TW, D], F32, kind="ExternalInput").ap()
    w_in = nc.dram_tensor("w_in", [D, INC], F32, kind="ExternalInput").ap()
    w_out = nc.dram_tensor("w_out", [D, D], F32, kind="ExternalInput").ap()
    w_gate = nc.dram_tensor("w_gate", [128, 128] if small_w else [D, DFF], F32, kind="ExternalInput").ap()
    w_up = nc.dram_tensor("w_up", [128, 128] if small_w else [D, DFF], F32, kind="ExternalInput").ap()
    w_down = nc.dram_tensor("w_down", [128, 128] if small_w else [DFF, D], F32, kind="ExternalInput").ap()
    prm = nc.dram_tensor("prm", [NPR, CW], F32, kind="ExternalInput").ap()
    gns = nc.dram_tensor("gns", [96, 128], F32, kind="ExternalInput").ap()
    fng = nc.dram_tensor("fng", [1, D], F32, kind="ExternalInput").ap()
    cident = nc.dram_tensor("cident", [128, 128], F32, kind="ExternalInput").ap()
    cmask = nc.dram_tensor("cmask", [128, 128], F32, kind="ExternalInput").ap()
    selm_d = nc.dram_tensor("selm", [128, 32], F32, kind="ExternalInput").ap()
    GR = 17 * 128
    gin = nc.dram_tensor("gin", [2 * GR, 128], F32)
    gout = nc.dram_tensor("gout", [NCORES * 2 * GR, 128], F32)
    out = nc.dram_tensor("out", [NH * T, D], F32, kind="ExternalOutput").ap()

    es = ExitStack()
    with es:
        P = Prog(nc, es)
        off = [0]

        sbc = {}

        def sb(name, shape, dt, at=None):
            if name in sbc:
                return sbc[name]
            sbc[name] = _sb(name, shape, dt, at)
            return sbc[name]

        def _sb(name, shape, dt, at=None):
            if at is None:
                at = off[0]
                nb = int(np.prod(shape[1:])) * (4 if dt == F32 else 2)
                off[0] = at + ((nb + 31) // 32) * 32
            return nc.alloc_sbuf_tensor_at(name, shape, dt, offset=16512 + at)

        PC = sb("PC", [128, NCB, NPR], F32)
        GN = sb("GN", [128, 96], F32)
        identb = sb("identb", [128, 128], BF16)
        identf = sb("identf", [128, 128], F32)
        maskb = sb("maskb", [128, TT, 128], BF16)
        onesf = sb("onesf", [128, 128], F32)
        stat = sb("stat", [128, 16], F32)
        lbv = sb("lbv", [128, NCB, 2], F32)
        dvec = sb("dvec", [128, 128], F32)
        dvec2 = sb("dvec2", [128, 128], F32)
        selm = sb("selm_sb", [128, 32], F32)
        assert off[0] <= 8192, off[0]
        OFF_HT = 8192
        hT = sb("hT", [128, DC, TW], BF16, at=OFF_HT)
        OFF_SP = OFF_HT + DC * T * 2
        OFF_RING = OFF_HT + DC * TW * 2
        SLOT = 16384
        ring = [sb(f"ring{i}", [128, SLOT // 2], BF16, at=OFF_RING + i * SLOT) for i in range(4)]
        OFF_X1 = OFF_RING + 4 * SLOT
        x1 = sb("x1", [128, TT, D], F32, at=OFF_X1)
        OFF_AB = OFF_X1 + TT * D * 4
        ring += [sb(f"ring{i}", [128, SLOT // 2], BF16, at=OFF_AB + (i - 4) * SLOT) for i in (4, 5)]
        OFF_U = OFF_AB + 2 * SLOT
        ubuf = [sb(f"u{i}", [128, 2, T], BF16, at=OFF_U + i * 2048) for i in range(2)]
        assert 16512 + OFF_U + 4096 <= 229344
        h2T = sb("h2T", [128, DC, T], BF16, at=OFF_HT)
        sgt = [sb(f"sgt{i}", [128, T], BF16, at=OFF_SP + i * 1024) for i in range(2)]
        hb_main = sb("hb", [128, D], BF16, at=OFF_AB + SLOT)
        xstage = sb("xstage", [128, D], F32, at=OFF_RING)
        hb2 = sb("hb2", [128, D], BF16, at=OFF_RING + SLOT)
        gb = sb("gb", [128, D], F32, at=OFF_RING)
        junk = sb("junk", [128, D], BF16, at=OFF_RING + SLOT)

        ps = [es.enter_context(nc.psum_tensor(f"ps{i}", [128, 512], F32)) for i in range(8)]

        B = {}

        def buf(name):
            if name not in B:
                B[name] = Buf(name)
            return B[name]

        bps = [buf(f"ps{i}") for i in range(8)]
        bring = [buf(f"ring{i}") for i in range(6)]

        P.op("sp", lambda e: e.dma_start(out=identf[:, :], in_=cident[:, :]), writes=[buf("identf")], dma="c0")
        P.op("sp", lambda e: e.dma_start(out=onesf[:, :], in_=cmask[:, :]), writes=[buf("onesf")], dma="c1")
        P.op("dve", lambda e: e.tensor_copy(out=identb[:, :], in_=identf[:, :]), reads=[buf("identf")], writes=[buf("identb")])
        P.op("dve", [lambda e, t=t: e.tensor_copy(out=maskb[:, t, :], in_=onesf[:, :]) for t in range(TT)],
             reads=[buf("onesf")], writes=[buf("maskb")])
        P.op("dve", lambda e: e.memset(onesf[:, :], 1.0), reads=[buf("maskb")], writes=[buf("onesf")])
        P.op("sp", lambda e: e.dma_start(out=selm[:, :], in_=selm_d[:, :]), writes=[buf("selm")], dma="c3")
        P.op("dve", lambda e: e.memset(dvec[:, :], 0.0), writes=[buf("dvec")])
        P.op("dve", lambda e: e.memset(dvec2[:, :], 0.0), writes=[buf("dvec2")])
        gstage = sb("gstage", [96, 128], F32, at=OFF_X1)
        P.op("sp", lambda e: e.dma_start(out=gstage[:, :], in_=gns[:, :]), writes=[buf("x1_0")], dma="x0")
        P.op("pe", lambda e: e.transpose(out=ps[0][:, 0:96], in_=gstage[:, :], identity=identf[0:96, 0:96]),
             reads=[buf("x1_0"), buf("identf")], writes=[bps[0]])
        P.op("dve", lambda e: e.tensor_copy(out=GN[:, :], in_=ps[0][:, 0:96]), reads=[bps[0]], writes=[buf("GN")])

        def ffn_ring_state():
            return {"i": 0}

        def norm_transpose(src_tile_ap, rows, dstT, tok0, gcol, srcbufs, dstbuf, tag, hbt=None):
            st = buf("stat")
            hb, bhb = (hb_main, bring[5]) if hbt is None else hbt
            P.op("act", lambda e: e.activation(out=hb[0:rows, :], in_=src_tile_ap, func=AF.Square,
                                               accum_out=stat[0:rows, 0:1]),
                 reads=srcbufs, writes=[bhb, st])
            P.op("dve", lambda e: e.tensor_scalar(out=stat[0:rows, 1:2], in0=stat[0:rows, 0:1], scalar1=1.0 / D,
                                                  scalar2=EPS, op0=ALU.mult, op1=ALU.add),
                 reads=[st], writes=[st])
            P.op("act", lambda e: e.activation(out=stat[0:rows, 2:3], in_=stat[0:rows, 1:2], func=AF.Sqrt),
                 reads=[st], writes=[st])
            P.op("dve", lambda e: e.reciprocal(out=stat[0:rows, 3:4], in_=stat[0:rows, 2:3]), reads=[st], writes=[st])
            P.op("act", lambda e: e.activation(out=hb[0:rows, :], in_=src_tile_ap, func=AF.Copy,
                                               scale=stat[0:rows, 3:4]),
                 reads=srcbufs + [st], writes=[bhb])
            for g8 in range(DC // 8):
                pb = bps[4 + (g8 % 2)]
                pst = ps[4 + (g8 % 2)].bitcast(BF16)
                P.op("pe", [lambda e, j=j, g8=g8, pst=pst: e.transpose(
                    out=pst[:, j * 128:j * 128 + rows], in_=hb[0:rows, (g8 * 8 + j) * 128:(g8 * 8 + j + 1) * 128],
                    identity=identb[0:rows, 0:rows]) for j in range(8)],
                    reads=[bhb, buf("identb")], writes=[pb])
                src3 = pst[:, :].rearrange("p (j t) -> p j t", t=128)[:, :, 0:rows]
                g3 = GN[:, gcol + g8 * 8: gcol + g8 * 8 + 8].unsqueeze(2).to_broadcast([128, 8, rows])
                P.op("dve", lambda e, g8=g8, src3=src3, g3=g3: e.tensor_tensor(
                    out=dstT[:, g8 * 8:g8 * 8 + 8, tok0:tok0 + rows], in0=src3, in1=g3, op=ALU.mult),
                    reads=[pb, buf("GN")], writes=[dstbuf])

        ring_i = [0]

        def next_slot():
            i = ring_i[0]
            ring_i[0] = (i + 1) % 6
            return i

        ring4_i = [0]

        def next_slot4():
            i = ring4_i[0]
            ring4_i[0] = (i + 1) % 4
            return i

        def next_pair4():
            i = ring4_i[0]
            if i % 2 == 1:
                i = (i + 1) % 4
            ring4_i[0] = (i + 2) % 4
            return i, i + 1

        class _Stop(Exception):
            pass

        def cut(name):
            if stop_after == name:
                raise _Stop()

        def half(hf):
            bx1 = [buf(f"x1_{t}") for t in range(TT)]
            def load_x_tiles():
                for t in range(TT):
                    r0 = hf * TW + HALO + t * 128
                    P.op("sp", lambda e, t=t, r0=r0: e.dma_start(out=x1[:, t, :], in_=xs[r0:r0 + 128, :]),
                         writes=[bx1[t]], dma=f"x{t}")

            if not do_attn:
                load_x_tiles()

            if do_attn:
                def fence(src, dst):
                    for d_ in dst:
                        for s_ in src:
                            if s_.w is not None and d_.r.get(s_.w[0], 0) < s_.w[1]:
                                d_.r[s_.w[0]] = s_.w[1]
                            for k_, v_ in s_.r.items():
                                if d_.r.get(k_, 0) < v_:
                                    d_.r[k_] = v_

                XO = OFF_X1
                y = sb("y", [128, NCB, T], F32, at=XO)
                oloc = sb("oloc", [128, HEADS, T], BF16, at=XO)
                gsT = sb("gsT", [128, HEADS, T], BF16, at=XO + 16384)
                TO = XO + 32768
                sgc = [sb(f"sgc{i}", [128, TW], F32, at=TO + i * 2176) for i in range(2)]
                hgc = [sb(f"hgc{i}", [128, TW], F32, at=TO + 4352 + i * 2176) for i in range(2)]
                sqt = [sb(f"sqt{i}", [128, T], F32, at=TO + 8704 + i * 2048) for i in range(2)]
                mt = sb("mt", [128, T], F32, at=TO + 12800)
                rstdt = sb("rstdt", [128, T], F32, at=TO + 14848)
                nmrt = sb("nmrt", [128, T], F32, at=TO + 16896)
                y2 = [sb("y2_0", [128, T], F32, at=TO + 12800), sb("y2_1", [128, T], F32, at=TO + 14848)]
                sgm = sb("sgm", [128, T], F32, at=TO)
                fch = sb("fch", [128, T], F32, at=TO + 2048)
                pch = sb("pch", [128, T], F32, at=TO + 4096)
                pseg = sb("pseg", [128, T], F32, at=TO + 6144)
                rp = sb("rp", [128, T], F32, at=TO + 8192)
                qs = sb("qs", [128, T], F32, at=TO + 10240)
                qt = sb("qt", [128, T], BF16, at=TO + 12288)
                kt = sb("kt", [128, T], BF16, at=TO + 13312)
                ktok = sb("ktok", [128, T], BF16, at=TO + 14336)
                vtok = sb("vtok", [128, TT, 128], BF16, at=TO + 15360)
                scm = sb("scm", [128, TT, 128], BF16, at=TO + 16384)
                Sst = [sb(f"Sst{i}", [128, 128], F32, at=TO + 17408 + i * 512) for i in range(2)]
                tmpS = sb("tmpS", [128, 128], F32, at=TO + 18432)
                Sb = sb("Sb", [128, 8, 128], BF16, at=TO + 18944)
                vE = sb("vE", [128, TT, 128], BF16, at=TO + 20992)
                vO = sb("vO", [128, TT, 128], BF16, at=TO + 22016)
                of_ = sb("of_", [128, T], F32, at=TO)
                sqf = sb("sqf", [128, T], F32, at=TO + 2048)
                rsf = sb("rsf", [128, T], F32, at=TO + 4096)
                Lr = [sb(f"Lr{i}", [128, HEADS, 128], F32, at=TO + i * 8192) for i in range(2)]
                Sacc = sb("Sacc", [128, HEADS, 128], F32, at=TO + 16384)
                dall = [sb(f"dall{i}", [128, NCORES, HEADS], F32, at=TO + 24576 + i * 512) for i in range(2)]
                deff = sb("deff", [128, HEADS], F32, at=TO + 25600)
                SinB = sb("SinB", [128, HEADS, 128], BF16, at=TO + 25664)
                a_out = sb("a_out", [128, NCB, T], BF16, at=OFF_AB)
                b_out = sb("b_out", [128, HEADS, T], BF16, at=OFF_AB + SLOT)
                xh = sb("xh", [32, D], F32, at=OFF_AB)
                pstage = sb("pstage", [NPR, CW], F32, at=OFF_X1 + 16384)

                if hf == 0:
                    P.op("sp", lambda e: e.dma_start(out=pstage[:, :], in_=prm[:, :]), writes=[bx1[1]], dma="x1")
                    for half8 in range(2):
                        P.op("pe", [lambda e, cb=cb, half8=half8: e.transpose(
                            out=ps[1 + half8][:, (cb % 8) * NPR:(cb % 8 + 1) * NPR], in_=pstage[:, cb * 128:(cb + 1) * 128],
                            identity=identf[0:NPR, 0:NPR]) for cb in range(half8 * 8, half8 * 8 + 8)],
                            reads=[bx1[1], buf("identf")], writes=[bps[1 + half8]])
                        P.op("dve", lambda e, half8=half8: e.tensor_copy(
                            out=PC[:, half8 * 8:half8 * 8 + 8, :],
                            in_=ps[1 + half8][:, 0:8 * NPR].rearrange("p (c r) -> p c r", r=NPR)),
                            reads=[bps[1 + half8]], writes=[buf("PC")])
                    P.op("dve", lambda e: e.tensor_tensor(out=lbv[:, :, 0], in0=PC[:, :, 34], in1=PC[:, :, 35], op=ALU.subtract),
                         reads=[buf("PC")], writes=[buf("lbv")])
                    P.op("act", lambda e: e.activation(out=lbv[:, :, 0], in_=lbv[:, :, 0], func=AF.Sigmoid),
                         reads=[buf("lbv")], writes=[buf("lbv")])
                    P.op("dve", lambda e: e.tensor_scalar(out=lbv[:, :, 1], in0=lbv[:, :, 0], scalar1=-1.0, scalar2=1.0,
                                                          op0=ALU.mult, op1=ALU.add), reads=[buf("lbv")], writes=[buf("lbv")])

                cut("params")
                load_x_tiles()
                P.op("sp", lambda e, hf=hf: e.dma_start(out=xh[:, :], in_=xs[hf * TW:hf * TW + HALO, :]),
                     writes=[bring[4]], dma="xh")
                bhT = buf("hT")
                norm_transpose(xh[:, :], HALO, hT, 0, 0, [bring[4]], bhT, "h")
                for t in range(TT):
                    norm_transpose(x1[:, t, :], 128, hT, HALO + t * 128, 0, [bx1[t]], bhT, "a")

                cut("a0")
                by = [buf(f"y{cb}") for cb in range(NCB)]
                ba = [buf(f"a{cb}") for cb in range(NCB)]
                bsgc = [buf("sgc0"), buf("sgc1")]
                bhgc = [buf("hgc0"), buf("hgc1")]
                bsqt = [buf("sqt0"), buf("sqt1")]
                ctemps = by + bsgc + bhgc + bsqt + [buf("mt"), buf("rstdt"), buf("nmrt"), buf("y2_0"), buf("y2_1")]
                fence(bx1, ctemps)
                fence([bring[4]], ba)
                pending_stats = []
                for cg in range(NCB // 2):
                    sv_, sg2_ = next_slot4(), next_slot4()
                    vsl = ring[sv_][:, :].rearrange("p (dc c) -> p dc c", c=256)
                    gsl2 = ring[sg2_][:, :].rearrange("p (dc c) -> p dc c", c=256)
                    P.op("pool", lambda e, cg=cg, vsl=vsl: e.dma_start(
                        out=vsl, in_=w_in[:, cg * 256:(cg + 1) * 256].rearrange("(dc p) c -> p dc c", p=128)),
                        writes=[bring[sv_]], dma=f"r{sv_}")
                    P.op("pool", lambda e, cg=cg, gsl2=gsl2: e.dma_start(
                        out=gsl2, in_=w_in[:, CW + cg * 256:CW + (cg + 1) * 256].rearrange("(dc p) c -> p dc c", p=128)),
                        writes=[bring[sg2_]], dma=f"r{sg2_}")
                    for cbl in range(2):
                        cb = 2 * cg + cbl
                        pA, pB, pC = (0, 1, 2) if cb % 2 == 0 else (3, 4, 5)
                        i2 = cb % 2
                        P.op("pe", [lambda e, dc=dc, cbl=cbl, pA=pA, vsl=vsl: e.matmul(
                            ps[pA][:, :], lhsT=vsl[:, dc, cbl * 128:(cbl + 1) * 128], rhs=hT[:, dc, HALO:TW],
                            start=(dc == 0), stop=(dc == DC - 1)) for dc in range(DC)],
                            reads=[bring[sv_], bhT], writes=[bps[pA]])
                        P.op("pe", [lambda e, dc=dc, cbl=cbl, pB=pB, gsl2=gsl2: e.matmul(
                            ps[pB][:, :], lhsT=gsl2[:, dc, cbl * 128:(cbl + 1) * 128], rhs=hT[:, dc, HALO:TW],
                            start=(dc == 0), stop=(dc == DC - 1)) for dc in range(DC)],
                            reads=[bring[sg2_], bhT], writes=[bps[pB]])
                        P.op("pe", [lambda e, dc=dc, cbl=cbl, pC=pC, vsl=vsl: e.matmul(
                            ps[pC][:, 0:HALO], lhsT=vsl[:, dc, cbl * 128:(cbl + 1) * 128], rhs=hT[:, dc, 0:HALO],
                            start=(dc == 0), stop=(dc == DC - 1)) for dc in range(DC)] +
                            [lambda e, dc=dc, cbl=cbl, pC=pC, gsl2=gsl2: e.matmul(
                            ps[pC][:, HALO:2 * HALO], lhsT=gsl2[:, dc, cbl * 128:(cbl + 1) * 128], rhs=hT[:, dc, 0:HALO],
                            start=(dc == 0), stop=(dc == DC - 1)) for dc in range(DC)],
                            reads=[bring[sv_], bring[sg2_], bhT], writes=[bps[pC]])
                        while len(pending_stats) > 0:
                            pending_stats.pop(0)()
                        P.op("act", [lambda e, pB=pB, i2=i2: e.activation(out=sgc[i2][:, HALO:TW], in_=ps[pB][:, :], func=AF.Sigmoid),
                                     lambda e, pC=pC, i2=i2: e.activation(out=sgc[i2][:, 0:HALO], in_=ps[pC][:, HALO:2 * HALO], func=AF.Sigmoid)],
                             reads=[bps[pB], bps[pC]], writes=[bsgc[i2]])
                        P.op("dve", [lambda e, pA=pA, i2=i2: e.tensor_tensor(out=hgc[i2][:, HALO:TW], in0=ps[pA][:, :], in1=sgc[i2][:, HALO:TW], op=ALU.mult),
                                     lambda e, pC=pC, i2=i2: e.tensor_tensor(out=hgc[i2][:, 0:HALO], in0=ps[pC][:, 0:HALO], in1=sgc[i2][:, 0:HALO], op=ALU.mult)],
                             reads=[bps[pA], bps[pC], bsgc[i2]], writes=[bhgc[i2]])
                        P.op("dve", lambda e, cb=cb, i2=i2: e.tensor_scalar(
                            out=y[:, cb, :], in0=hgc[i2][:, 2:2 + T], scalar1=PC[:, cb, 0:1], scalar2=PC[:, cb, 31:32],
                            op0=ALU.mult, op1=ALU.add), reads=[bhgc[i2], buf("PC")], writes=[by[cb]])
                        by2 = buf(f"y2_{i2}")
                        P.op("dve", lambda e, cb=cb, i2=i2: e.tensor_scalar(
                            out=y2[i2][:, :], in0=hgc[i2][:, 3:3 + T], scalar1=PC[:, cb, 1:2], scalar2=None, op0=ALU.mult),
                            reads=[bhgc[i2], buf("PC")], writes=[by2])
                        for j in range(2, KS):
                            if j % 2 == 0:
                                P.op("dve", lambda e, cb=cb, i2=i2, j=j: e.scalar_tensor_tensor(
                                    out=y[:, cb, :], in0=hgc[i2][:, 2 + j:2 + j + T], scalar=PC[:, cb, j:j + 1], in1=y[:, cb, :],
                                    op0=ALU.mult, op1=ALU.add), reads=[bhgc[i2], by[cb]], writes=[by[cb]])
                            else:
                                P.op("dve", lambda e, cb=cb, i2=i2, j=j: e.scalar_tensor_tensor(
                                    out=y2[i2][:, :], in0=hgc[i2][:, 2 + j:2 + j + T], scalar=PC[:, cb, j:j + 1], in1=y2[i2][:, :],
                                    op0=ALU.mult, op1=ALU.add), reads=[bhgc[i2], by2], writes=[by2])
                        P.op("dve", lambda e, cb=cb, i2=i2: e.tensor_tensor(
                            out=y[:, cb, :], in0=y[:, cb, :], in1=y2[i2][:, :], op=ALU.add),
                            reads=[by[cb], by2], writes=[by[cb]])
                        P.op("act", lambda e, cb=cb, i2=i2: e.activation(out=sqt[i2][:, :], in_=y[:, cb, :], func=AF.Square),
                             reads=[by[cb]], writes=[bsqt[i2]])
                        def stats(cb=cb, i2=i2):
                            P.op("pe", lambda e: e.matmul(ps[6][:, :], lhsT=onesf[:, :], rhs=y[:, cb, :],
                                                          start=(cb == 0), stop=(cb == NCB - 1)),
                                 reads=[buf("onesf"), by[cb]], writes=[bps[6]])
                            P.op("pe", lambda e: e.matmul(ps[7][:, :], lhsT=onesf[:, :], rhs=sqt[i2][:, :],
                                                          start=(cb == 0), stop=(cb == NCB - 1)),
                                 reads=[buf("onesf"), bsqt[i2]], writes=[bps[7]])
                        pending_stats.append(stats)
                while len(pending_stats) > 0:
                    pending_stats.pop(0)()
                cut("convmm")
                bmt, brs, bnm = buf("mt"), buf("rstdt"), buf("nmrt")
                fence([buf("y2_0"), buf("y2_1")], [bmt, brs])
                P.op("dve", lambda e: e.tensor_scalar(out=mt[:, :], in0=ps[6][:, :], scalar1=1.0 / CW, scalar2=None, op0=ALU.mult),
                     reads=[bps[6]], writes=[bmt])
                P.op("dve", lambda e: e.tensor_tensor(out=nmrt[:, :], in0=mt[:, :], in1=mt[:, :], op=ALU.mult),
                     reads=[bmt], writes=[bnm])
                P.op("dve", lambda e: e.scalar_tensor_tensor(out=rstdt[:, :], in0=ps[7][:, :], scalar=1.0 / CW, in1=nmrt[:, :],
                                                             op0=ALU.mult, op1=ALU.subtract), reads=[bps[7], bnm], writes=[brs])
                P.op("dve", lambda e: e.tensor_scalar(out=rstdt[:, :], in0=rstdt[:, :], scalar1=EPS, scalar2=None, op0=ALU.add),
                     reads=[brs], writes=[brs])
                P.op("act", lambda e: e.activation(out=rstdt[:, :], in_=rstdt[:, :], func=AF.Sqrt), reads=[brs], writes=[brs])
                P.op("dve", lambda e: e.reciprocal(out=rstdt[:, :], in_=rstdt[:, :]), reads=[brs], writes=[brs])
                P.op("dve", lambda e: e.scalar_tensor_tensor(out=nmrt[:, :], in0=mt[:, :], scalar=-1.0, in1=rstdt[:, :],
                                                             op0=ALU.mult, op1=ALU.mult), reads=[bmt, brs], writes=[bnm])
                for cb in range(NCB):
                    P.op("dve", lambda e, cb=cb: e.tensor_tensor(out=y[:, cb, :], in0=y[:, cb, :], in1=rstdt[:, :], op=ALU.mult),
                         reads=[by[cb], brs], writes=[by[cb]])
                    P.op("dve", lambda e, cb=cb: e.tensor_tensor(out=y[:, cb, :], in0=y[:, cb, :], in1=nmrt[:, :], op=ALU.add),
                         reads=[by[cb], bnm], writes=[by[cb]])
                    P.op("act", lambda e, cb=cb: e.activation(out=a_out[:, cb, :], in_=y[:, cb, :], func=AF.Silu,
                                                              scale=PC[:, cb, 32:33], bias=PC[:, cb, 33:34]),
                         reads=[by[cb], buf("PC")], writes=[ba[cb]])

                cut("conv")
                bb = [buf(f"b{h}") for h in range(HEADS)]
                bol = [buf(f"ol{h}") for h in range(HEADS)]
                bgs = [buf(f"gs{h}") for h in range(HEADS)]
                hn = ["sgm", "fch", "pch", "pseg", "rp", "qs", "qt", "kt", "ktok", "vtok", "scm", "Sst0", "Sst1", "tmpS", "Sb", "vE", "vO"]
                hb_ = {n: buf("h_" + n) for n in hn}
                fence(ctemps, list(hb_.values()) + bol + bgs)
                P.op("dve", lambda e: e.memset(vE[64:128, :, :], 0.0), writes=[hb_["vE"]])
                P.op("dve", lambda e: e.memset(vO[0:64, :, :], 0.0), writes=[hb_["vO"]])
                fence([bring[5]], bb)
                def head_pass(h, so):
                    N = (lambda *a, **k: None) if so else P.op
                    s1_, s2_ = next_slot4(), next_slot4()
                    sl1 = ring[s1_][:, :].rearrange("p (dc two c) -> p dc two c", two=2, c=128)
                    sl2 = ring[s2_][:, :].rearrange("p (dc two c) -> p dc two c", two=2, c=128)
                    c_q, c_f, c_i, c_g = (2 * CW + h * 128, 3 * CW + h * 128, 4 * CW + h * 128, 5 * CW + h * 128)
                    P.op("pool", [lambda e, sl1=sl1, c0=c0, w=w: e.dma_start(
                        out=sl1[:, :, w, :], in_=w_in[:, c0:c0 + 128].rearrange("(dc p) c -> p dc c", p=128))
                        for w, c0 in (((1, c_f),) if so else ((0, c_q), (1, c_f)))], writes=[bring[s1_]], dma=f"r{s1_}")
                    P.op("pool", [lambda e, sl2=sl2, c0=c0, w=w: e.dma_start(
                        out=sl2[:, :, w, :], in_=w_in[:, c0:c0 + 128].rearrange("(dc p) c -> p dc c", p=128))
                        for w, c0 in (((0, c_i),) if so else ((0, c_i), (1, c_g)))], writes=[bring[s2_]], dma=f"r{s2_}")
                    for pi, sl, w, sidx in (((1, sl1, 1, s1_),) if so else ((0, sl1, 0, s1_), (1, sl1, 1, s1_), (2, sl2, 1, s2_))):
                        P.op("pe", [lambda e, dc=dc, sl=sl, w=w, pi=pi: e.matmul(
                            ps[pi][:, :], lhsT=sl[:, dc, w, :], rhs=hT[:, dc, HALO:TW],
                            start=(dc == 0), stop=(dc == DC - 1)) for dc in range(DC)],
                            reads=[bring[sidx], bhT], writes=[bps[pi]])
                    P.op("pe", [lambda e, dc=dc, tt=tt, sl2=sl2: e.matmul(
                        ps[3][:, tt * 128:(tt + 1) * 128], lhsT=hT[:, dc, HALO + tt * 128:HALO + (tt + 1) * 128],
                        rhs=sl2[:, dc, 0, :], start=(dc == 0), stop=(dc == DC - 1)) for tt in range(TT) for dc in range(DC)],
                        reads=[bring[s2_], bhT], writes=[bps[3]])
                    H = hb_
                    P.op("act", lambda e: e.activation(out=sgm[:, :], in_=ps[1][:, :], func=AF.Sigmoid, scale=-1.0),
                         reads=[bps[1]], writes=[H["sgm"]])
                    P.op("dve", lambda e, h=h: e.tensor_scalar(out=sgm[:, :], in0=sgm[:, :], scalar1=lbv[:, h, 1:2], scalar2=None,
                                                               op0=ALU.mult), reads=[H["sgm"], buf("lbv")], writes=[H["sgm"]])
                    P.op("dve", lambda e: e.tensor_scalar(out=fch[:, :], in0=sgm[:, :], scalar1=-1.0, scalar2=1.0,
                                                          op0=ALU.mult, op1=ALU.add), reads=[H["sgm"]], writes=[H["fch"]])
                    P.op("dve", lambda e: e.tensor_tensor_scan(out=pseg[:, :], data0=fch[:, :], data1=fch[:, :], initial=1.0,
                                                               op0=ALU.mult, op1=ALU.min), reads=[H["fch"]], writes=[H["pseg"]])
                    P.op("dve", [lambda e, j=j: e.tensor_tensor_scan(
                        out=pch[:, j * 64:(j + 1) * 64], data0=fch[:, j * 64:(j + 1) * 64], data1=fch[:, j * 64:(j + 1) * 64],
                        initial=1.0, op0=ALU.mult, op1=ALU.min) for j in range(8)], reads=[H["fch"]], writes=[H["pch"]])
                    N("act", lambda e: e.activation(out=qs[:, :], in_=ps[0][:, :], func=AF.Silu), reads=[bps[0]], writes=[H["qs"]])
                    N("dve", lambda e: e.tensor_tensor(out=qt[:, :], in0=qs[:, :], in1=pch[:, :], op=ALU.mult),
                         reads=[H["qs"], H["pch"]], writes=[H["qt"]])
                    N("dve", lambda e, h=h: e.tensor_tensor(out=b_out[:, h, :], in0=qs[:, :], in1=pseg[:, :], op=ALU.mult),
                         reads=[H["qs"], H["pseg"]], writes=[bb[h]])
                    P.op("dve", lambda e: e.tensor_scalar(out=rp[:, :], in0=pch[:, :], scalar1=1e-30, scalar2=None, op0=ALU.max),
                         reads=[H["pch"]], writes=[H["rp"]])
                    P.op("dve", lambda e: e.reciprocal(out=rp[:, :], in_=rp[:, :]), reads=[H["rp"]], writes=[H["rp"]])
                    P.op("dve", lambda e: e.tensor_tensor(out=kt[:, :], in0=sgm[:, :], in1=rp[:, :], op=ALU.mult),
                         reads=[H["sgm"], H["rp"]], writes=[H["kt"]])
                    N("act", lambda e, h=h: e.activation(out=gsT[:, h, :], in_=ps[2][:, :], func=AF.Silu),
                         reads=[bps[2]], writes=[bgs[h]])
                    N("act", lambda e: e.activation(out=vtok[:, :, :], in_=ps[3][:, :].rearrange("p (t v) -> p t v", v=128),
                                                       func=AF.Copy), reads=[bps[3]], writes=[H["vtok"]])
                    P.op("act", lambda e: e.activation(out=vE[0:64, :, :], in_=ps[3][0:64, :].rearrange("p (t v) -> p t v", v=128),
                                                       func=AF.Copy), reads=[bps[3]], writes=[H["vE"]])
                    P.op("act", lambda e: e.activation(out=vO[64:128, :, :], in_=ps[3][64:128, :].rearrange("p (t v) -> p t v", v=128),
                                                       func=AF.Copy), reads=[bps[3]], writes=[H["vO"]])
                    pst4 = ps[4].bitcast(BF16)
                    P.op("pe", [lambda e, tt=tt, pst4=pst4: e.transpose(out=pst4[:, tt * 128:(tt + 1) * 128],
                                                                        in_=kt[:, tt * 128:(tt + 1) * 128], identity=identb[:, :])
                                for tt in range(TT)], reads=[H["kt"], buf("identb")], writes=[bps[4]])
                    P.op("act", lambda e, pst4=pst4: e.activation(out=ktok[:, :], in_=pst4[:, 0:T], func=AF.Copy),
                         reads=[bps[4]], writes=[H["ktok"]])
                    N("pe", [lambda e, tt=tt: e.matmul(ps[5][:, tt * 128:(tt + 1) * 128], lhsT=kt[:, tt * 128:(tt + 1) * 128],
                                                          rhs=qt[:, tt * 128:(tt + 1) * 128], start=True, stop=True)
                                for tt in range(TT)], reads=[H["kt"], H["qt"]], writes=[bps[5]])
                    N("dve", lambda e: e.tensor_tensor(out=scm[:, :, :], in0=ps[5][:, :].rearrange("p (t v) -> p t v", v=128),
                                                          in1=maskb[:, :, :], op=ALU.mult),
                         reads=[bps[5], buf("maskb")], writes=[H["scm"]])
                    for rnd in range(2):
                        P.op("pe", [lambda e, j=j: e.matmul(
                            ps[7][:, (j % 4) * 128:(j % 4 + 1) * 128],
                            lhsT=ktok[:, (j // 2) * 128:(j // 2 + 1) * 128],
                            rhs=(vE if j % 2 == 0 else vO)[:, j // 2, :], start=True, stop=True)
                            for j in range(rnd * 4, rnd * 4 + 4)], reads=[H["ktok"], H["vE"], H["vO"]], writes=[bps[7]])
                        for j in range(rnd * 4, rnd * 4 + 4):
                            cur, nxt = Sst[j % 2], Sst[(j + 1) % 2]
                            bcur, bnxt = H[f"Sst{j % 2}"], H[f"Sst{(j + 1) % 2}"]
                            ej = pch[:, j * 64 + 63:j * 64 + 64]
                            uj = ps[7][:, (j % 4) * 128:(j % 4 + 1) * 128]
                            if j == 0:
                                P.op("dve", lambda e, nxt=nxt, ej=ej, uj=uj: e.tensor_scalar(
                                    out=nxt[:, :], in0=uj, scalar1=ej, scalar2=None, op0=ALU.mult),
                                    reads=[bps[7], H["pch"]], writes=[bnxt])
                            else:
                                P.op("dve", lambda e, cur=cur, uj=uj: e.tensor_tensor(out=tmpS[:, :], in0=uj, in1=cur[:, :], op=ALU.add),
                                     reads=[bps[7], bcur], writes=[H["tmpS"]])
                                P.op("dve", lambda e, nxt=nxt, ej=ej: e.tensor_scalar(
                                    out=nxt[:, :], in0=tmpS[:, :], scalar1=ej, scalar2=None, op0=ALU.mult),
                                    reads=[H["tmpS"], H["pch"]], writes=[bnxt])
                            if j < 7:
                                N("act", lambda e, nxt=nxt, j=j: e.activation(out=Sb[:, j + 1, :], in_=nxt[:, :], func=AF.Copy),
                                     reads=[bnxt], writes=[H["Sb"]])
                    for tt in range(TT):
                        fl = [lambda e, tt=tt: e.matmul(ps[6][:, tt * 128:(tt + 1) * 128], lhsT=vtok[:, tt, :], rhs=scm[:, tt, :],
                                                       start=True, stop=False)]
                        js = [j for j in (2 * tt, 2 * tt + 1) if j >= 1]
                        for j in js:
                            fl.append(lambda e, j=j, last=(j == js[-1]): e.matmul(
                                ps[6][:, j * 64:(j + 1) * 64], lhsT=Sb[:, j, :], rhs=qt[:, j * 64:(j + 1) * 64],
                                start=False, stop=last))
                        N("pe", fl, reads=[H["vtok"], H["scm"], H["Sb"], H["qt"]], writes=[bps[6]])
                    N("act", lambda e, h=h: e.activation(out=oloc[:, h, :], in_=ps[6][:, :], func=AF.Copy),
                         reads=[bps[6]], writes=[bol[h]])
                    if hf == 0:
                        g0r = (GR if so else 0) + h * 128
                        dv, bdv = (dvec2, buf("dvec2")) if so else (dvec, buf("dvec"))
                        P.op("sp", lambda e, g0r=g0r: e.dma_start(out=gin.ap()[g0r:g0r + 128, :], in_=Sst[0][:, :]),
                             reads=[H["Sst0"]], writes=[buf("gin")], dma="gi")
                        P.op("dve", lambda e, h=h, dv=dv: e.tensor_copy(out=dv[:, h:h + 1], in_=pseg[:, T - 1:T]),
                             reads=[H["pseg"]], writes=[bdv])

                for h in range(HEADS):
                    head_pass(h, False)
                if hf == 0 and nh == 2 and "gather" not in skip:
                    for t in range(TT):
                        r0 = TW + HALO + t * 128
                        P.op("sp", lambda e, r0=r0: e.dma_start(out=xstage[:, :], in_=xs[r0:r0 + 128, :]),
                             writes=[bring[0]], dma="xs0")
                        norm_transpose(xstage[:, :], 128, hT, HALO + t * 128, 0, [bring[0]], bhT, "p", hbt=(hb2, bring[1]))
                    for h in range(HEADS):
                        head_pass(h, True)

                cut("hgrn")
                bgin, bgout = buf("gin"), buf("gout")
                if hf == 0 and "gather" not in skip:
                    P.op("sp", [lambda e: e.dma_start(out=gin.ap()[2048:GR, :], in_=dvec[:, :]),
                                lambda e: e.dma_start(out=gin.ap()[GR + 2048:2 * GR, :], in_=dvec2[:, :])],
                         reads=[buf("dvec"), buf("dvec2")], writes=[bgin], dma="gi")
                    P.op("pool", lambda e: e.collective_compute(
                        "AllGather", ALU.bypass, replica_groups=[list(range(NCORES))],
                        ins=[gin.ap().opt()], outs=[gout.ap().opt()]),
                        reads=[bgin], writes=[bgout], cc="g0")
                bLr = [buf("Lr0"), buf("Lr1")]
                bSa, bda, bde, bSi = buf("Sacc"), [buf("dall0"), buf("dall1")], buf("deff"), buf("SinB")
                bof, bsq, brf = buf("of_"), buf("sqf"), buf("rsf")
                gtemps = bLr + [bSa, bde, bSi] + bda
                fence(list(hb_.values()), gtemps)
                Sacc2 = Sacc[:, :, :].rearrange("p h v -> p (h v)")
                P.op("dve", lambda e: e.memset(Sacc2, 0.0), writes=[bSa])
                step = 0
                hlist = list(range(hf + 1)) if "gather" not in skip else []
                g4 = gout.ap().rearrange("(r h k) c -> k r h c", r=NCORES, h=34, k=128)
                for hh in hlist:
                    P.op("sp", lambda e, hh=hh: e.dma_start(out=dall[hh][:, :, :], in_=g4[:, :, 17 * hh + 16, 0:HEADS]),
                         reads=[bgout], writes=[bda[hh]], dma=f"da{hh}")
                    mo = 0 if hh == hf else 16
                    for r in range(NCORES):
                        li = step % 2
                        step += 1
                        P.op("sp", lambda e, hh=hh, r=r, li=li: e.dma_start(out=Lr[li][:, :, :], in_=g4[:, r, 17 * hh:17 * hh + HEADS, :]),
                             reads=[bgout], writes=[bLr[li]], dma=f"L{li}")
                        P.op("dve", lambda e, hh=hh, r=r, mo=mo: e.tensor_scalar(
                            out=deff[:, :], in0=dall[hh][:, r, :], scalar1=selm[:, mo + r:mo + r + 1],
                            scalar2=selm[:, mo + 8 + r:mo + 9 + r], op0=ALU.mult, op1=ALU.add),
                            reads=[bda[hh], buf("selm")], writes=[bde])
                        P.op("dve", lambda e: e.tensor_tensor(
                            out=Sacc[:, :, :], in0=Sacc[:, :, :], in1=deff[:, :].unsqueeze(2).to_broadcast([128, HEADS, 128]),
                            op=ALU.mult), reads=[bde, bSa], writes=[bSa])
                        P.op("dve", lambda e, li=li, r=r, mo=mo: e.scalar_tensor_tensor(
                            out=Sacc2, in0=Lr[li][:, :, :].rearrange("p h v -> p (h v)"), scalar=selm[:, mo + r:mo + r + 1],
                            in1=Sacc2, op0=ALU.mult, op1=ALU.add), reads=[bLr[li], bSa, buf("selm")], writes=[bSa])
                P.op("act", lambda e: e.activation(out=SinB[:, :, :].rearrange("p h v -> p (h v)"), in_=Sacc2, func=AF.Copy),
                     reads=[bSa], writes=[bSi])

                fence(bLr, [bof, bsq, brf])
                for h in range(HEADS):
                    pq = 0 if h % 2 == 0 else 2
                    P.op("pe", lambda e, h=h, pq=pq: e.matmul(ps[pq + 1][:, :], lhsT=SinB[:, h, :], rhs=b_out[:, h, :],
                                                             start=True, stop=True),
                         reads=[bSi, bb[h]], writes=[bps[pq + 1]])
                    P.op("dve", lambda e, h=h, pq=pq: e.tensor_tensor(out=of_[:, :], in0=ps[pq + 1][:, :], in1=oloc[:, h, :], op=ALU.add),
                         reads=[bps[pq + 1], bol[h]], writes=[bof])
                    P.op("act", lambda e: e.activation(out=sqf[:, :], in_=of_[:, :], func=AF.Square), reads=[bof], writes=[bsq])
                    P.op("pe", lambda e, pq=pq: e.matmul(ps[pq][:, :], lhsT=onesf[:, :], rhs=sqf[:, :], start=True, stop=True),
                         reads=[buf("onesf"), bsq], writes=[bps[pq]])
                    P.op("dve", lambda e, pq=pq: e.tensor_scalar(out=rsf[:, :], in0=ps[pq][:, :], scalar1=1.0 / 128, scalar2=EPS,
                                                                 op0=ALU.mult, op1=ALU.add), reads=[bps[pq]], writes=[brf])
                    P.op("act", lambda e: e.activation(out=rsf[:, :], in_=rsf[:, :], func=AF.Sqrt), reads=[brf], writes=[brf])
                    P.op("dve", lambda e: e.reciprocal(out=rsf[:, :], in_=rsf[:, :]), reads=[brf], writes=[brf])
                    P.op("dve", lambda e: e.tensor_tensor(out=of_[:, :], in0=of_[:, :], in1=rsf[:, :], op=ALU.mult),
                         reads=[bof, brf], writes=[bof])
                    P.op("dve", lambda e, h=h: e.scalar_tensor_tensor(out=b_out[:, h, :], in0=of_[:, :], scalar=PC[:, h, 36:37],
                                                                      in1=gsT[:, h, :], op0=ALU.mult, op1=ALU.mult),
                         reads=[bof, bgs[h], buf("PC")], writes=[bb[h]])

                cut("fin")
                fence(ctemps + list(hb_.values()) + bol + bgs + [bof, bsq, brf] + gtemps, bx1)
                for t in range(TT):
                    r0 = hf * TW + HALO + t * 128
                    P.op("sp", lambda e, t=t, r0=r0: e.dma_start(out=x1[:, t, :], in_=xs[r0:r0 + 128, :]),
                         writes=[bx1[t]], dma=f"x{t}")
                for db in range(D // 512):
                    sA, sB = next_pair4()
                    wsl = [ring[sA][:, :].rearrange("p (mc c) -> p mc c", c=512), ring[sB][:, :].rearrange("p (mc c) -> p mc c", c=512)]
                    for q, sidx in ((0, sA), (1, sB)):
                        P.op("pool", lambda e, db=db, q=q, wsl=wsl: e.dma_start(
                            out=wsl[q], in_=w_out[q * 2048:(q + 1) * 2048, db * 512:(db + 1) * 512].rearrange("(mc p) c -> p mc c", p=128)),
                            writes=[bring[sidx]], dma=f"r{sidx}")
                    for t in range(TT):
                        pb = (db % 2) * 4 + t
                        P.op("pe", [lambda e, mc=mc, t=t, pb=pb, wsl=wsl: e.matmul(
                            ps[pb][:, :], lhsT=(a_out if mc < 16 else b_out)[:, mc % 16, t * 128:(t + 1) * 128],
                            rhs=wsl[mc // 16][:, mc % 16, :], start=(mc == 0), stop=(mc == 31)) for mc in range(32)],
                            reads=[bring[sA], bring[sB]] + ba + bb, writes=[bps[pb]])
                        P.op("dve", lambda e, t=t, db=db, pb=pb: e.tensor_tensor(
                            out=x1[:, t, db * 512:(db + 1) * 512], in0=ps[pb][:, :], in1=x1[:, t, db * 512:(db + 1) * 512], op=ALU.add),
                            reads=[bps[pb], bx1[t]], writes=[bx1[t]])
                fence(ba + bb, [bring[4], bring[5]])
                ring_i[0] = 0


            for t in range(TT):
                norm_transpose(x1[:, t, :], 128, h2T, t * 128, 32, [bx1[t]], buf("hT"), "f")
            for g in range(nfg if do_ffn else 0):
                sg_, su_, sd_ = next_slot(), next_slot(), next_slot()
                gsl = ring[sg_][:, :].rearrange("p (dc c) -> p dc c", c=256)
                usl = ring[su_][:, :].rearrange("p (dc c) -> p dc c", c=256)
                dsl = ring[sd_][:, :].rearrange("p (fc d) -> p fc d", d=D)
                P.op("pool", lambda e, g=g, gsl=gsl: e.dma_start(
                    out=gsl, in_=w_gate[:, g * 256:(g + 1) * 256].rearrange("(dc p) c -> p dc c", p=128)),
                    writes=[bring[sg_]], dma=f"r{sg_}")
                P.op("pool", lambda e, g=g, usl=usl: e.dma_start(
                    out=usl, in_=w_up[:, g * 256:(g + 1) * 256].rearrange("(dc p) c -> p dc c", p=128)),
                    writes=[bring[su_]], dma=f"r{su_}")
                P.op("pool", [lambda e, g=g, dsl=dsl, fc=fc, hh=hh: e.dma_start(
                    out=dsl[:, fc, hh * 2048:(hh + 1) * 2048],
                    in_=w_down[g * 256 + fc * 128:g * 256 + (fc + 1) * 128, hh * 2048:(hh + 1) * 2048])
                    for fc in range(2) for hh in range(2)],
                    writes=[bring[sd_]], dma=f"r{sd_}")
                ub = ubuf[g % 2]
                bu = buf(f"u{g % 2}")
                for fc in range(2):
                    pg, pu = 2 * fc, 2 * fc + 1
                    P.op("pe", [lambda e, dc=dc, fc=fc, pg=pg, gsl=gsl: e.matmul(
                        ps[pg][:, :], lhsT=gsl[:, dc, fc * 128:(fc + 1) * 128], rhs=h2T[:, dc, :],
                        start=(dc == 0), stop=(dc == DC - 1)) for dc in range(DC)],
                        reads=[bring[sg_], buf("hT")], writes=[bps[pg]])
                    P.op("pe", [lambda e, dc=dc, fc=fc, pu=pu, usl=usl: e.matmul(
                        ps[pu][:, :], lhsT=usl[:, dc, fc * 128:(fc + 1) * 128], rhs=h2T[:, dc, :],
                        start=(dc == 0), stop=(dc == DC - 1)) for dc in range(DC)],
                        reads=[bring[su_], buf("hT")], writes=[bps[pu]])
                    sgi = sgt[fc]
                    bsg = buf(f"sgt{fc}")
                    P.op("act", lambda e, pg=pg, sgi=sgi: e.activation(out=sgi[:, :], in_=ps[pg][:, :], func=AF.Silu),
                         reads=[bps[pg]], writes=[bsg])
                    P.op("dve", lambda e, pu=pu, sgi=sgi, fc=fc, ub=ub: e.tensor_tensor(
                        out=ub[:, fc, :], in0=ps[pu][:, :], in1=sgi[:, :], op=ALU.mult),
                        reads=[bps[pu], bsg], writes=[bu])
                k = 0
                for t in range(TT):
                    for db in range(D // 512):
                        pb = 4 + (k % 4)
                        k += 1
                        P.op("pe", [lambda e, fc=fc, t=t, db=db, pb=pb, ub=ub, dsl=dsl: e.matmul(
                            ps[pb][:, :], lhsT=ub[:, fc, t * 128:(t + 1) * 128], rhs=dsl[:, fc, db * 512:(db + 1) * 512],
                            start=(fc == 0), stop=(fc == 1)) for fc in range(2)],
                            reads=[bu, bring[sd_]], writes=[bps[pb]])
                        P.op("dve", lambda e, t=t, db=db, pb=pb: e.tensor_tensor(
                            out=x1[:, t, db * 512:(db + 1) * 512], in0=ps[pb][:, :],
                            in1=x1[:, t, db * 512:(db + 1) * 512], op=ALU.add),
                            reads=[bps[pb], bx1[t]], writes=[bx1[t]])

            st = buf("stat")
            allring = [bring[0], bring[1]]
            P.op("sp", lambda e: e.dma_start(out=gb[:, :], in_=fng[0:1, :].partition_broadcast(128)),
                 writes=[bring[0]], dma="gb")
            for t in range(TT):
                P.op("act", lambda e, t=t: e.activation(out=junk[:, :], in_=x1[:, t, :], func=AF.Square,
                                                       accum_out=stat[:, 0:1]),
                     reads=[bx1[t]], writes=[bring[1], st])
                P.op("dve", lambda e: e.tensor_scalar(out=stat[:, 1:2], in0=stat[:, 0:1], scalar1=1.0 / D,
                                                      scalar2=EPS, op0=ALU.mult, op1=ALU.add), reads=[st], writes=[st])
                P.op("act", lambda e: e.activation(out=stat[:, 2:3], in_=stat[:, 1:2], func=AF.Sqrt),
                     reads=[st], writes=[st])
                P.op("dve", lambda e: e.reciprocal(out=stat[:, 3:4], in_=stat[:, 2:3]), reads=[st], writes=[st])
                P.op("dve", lambda e, t=t: e.scalar_tensor_tensor(
                    out=x1[:, t, :], in0=x1[:, t, :], scalar=stat[:, 3:4], in1=gb[:, :], op0=ALU.mult, op1=ALU.mult),
                    reads=[bx1[t], st, bring[0]], writes=[bx1[t]])
                r0 = hf * T + t * 128
                P.op("sp", lambda e, t=t, r0=r0: e.dma_start(out=out[r0:r0 + 128, :], in_=x1[:, t, :]),
                     reads=[bx1[t]], dma=f"o{t}")
        for hf in range(nh):
            try:
                half(hf)
            except _Stop:
                break
        fin = []
        for t in range(TT):
            for k, v in buf(f"x1_{t}").r.items():
                fin.append((k, v))
        P.op("sp", lambda e: e.nop(), extra=fin)

        with nc.Block() as block:
            @block.tensor
            def _(e):
                P.replay("pe", e)

            @block.scalar
            def _(e):
                P.replay("act", e)

            @block.vector
            def _(e):
                P.replay("dve", e)

            @block.gpsimd
            def _(e):
                P.replay("pool", e)

            @block.sync
            def _(e):
                P.replay("sp", e)
    return nc


_NC_CACHE = {}


def _selm(c):
    m = (np.arange(NCORES) < c).astype(np.float32)
    row = np.concatenate([m, 1.0 - m, np.ones(NCORES, np.float32), np.zeros(NCORES, np.float32)])
    return np.ascontiguousarray(np.broadcast_to(row, (128, 32)).astype(np.float32))


def _host_layout(x, attn_norm_g, conv_w, conv_b, conv_ln_g, conv_ln_b, hg_lb_logits, hg_norm_g,
                 ffn_norm_g, final_norm_g):
    x2 = np.ascontiguousarray(x.reshape(SEQ, D))
    xpad = np.concatenate([np.zeros((HALO, D), np.float32), x2], axis=0)
    xs_list = []
    for c in range(NCORES):
        segs = [xpad[(c + 8 * h) * T:(c + 8 * h) * T + TW] for h in range(NH)]
        xs_list.append(np.ascontiguousarray(np.concatenate(segs, axis=0)))
    prm = np.zeros((NPR, CW), np.float32)
    prm[0:KS] = conv_w[0]
    prm[31] = conv_b[0]
    prm[32] = conv_ln_g[0]
    prm[33] = conv_ln_b[0]
    prm[34:36] = hg_lb_logits
    prm[36] = hg_norm_g[0]
    gns = np.concatenate([attn_norm_g[0].reshape(32, 128), ffn_norm_g[0].reshape(32, 128),
                          final_norm_g.reshape(32, 128)], axis=0).astype(np.float32)
    fng = np.ascontiguousarray(final_norm_g.reshape(1, D).astype(np.float32))
    return xs_list, prm, gns, fng


def kernel(x, attn_norm_g, w_in, conv_w, conv_b, conv_ln_g, conv_ln_b, hg_lb_logits,
           hg_norm_g, w_out, ffn_norm_g, w_gate, w_up, w_down, final_norm_g):
    f = lambda a: np.ascontiguousarray(np.asarray(a, dtype=np.float32))
    x = f(x)
    xs_list, prm, gns, fng = _host_layout(x, f(attn_norm_g), f(conv_w), f(conv_b), f(conv_ln_g), f(conv_ln_b),
                                          f(hg_lb_logits), f(hg_norm_g), f(ffn_norm_g), f(final_norm_g))
    cident = np.eye(128, dtype=np.float32)
    s = np.arange(128)
    cmask = ((s[:, None] <= s[None, :]) & ((s[:, None] // 64) == (s[None, :] // 64))).astype(np.float32)
    if "nc" not in _NC_CACHE:
        _NC_CACHE["nc"] = build_nc()
    nc = _NC_CACHE["nc"]
    shared = dict(w_in=f(w_in)[0], w_out=f(w_out)[0], w_gate=f(w_gate)[0], w_up=f(w_up)[0], w_down=f(w_down)[0],
                  prm=prm, gns=gns, fng=fng, cident=cident, cmask=cmask)
    in_maps = [dict(shared, xs=xs_list[c], selm=_selm(c)) for c in range(NCORES)]
    res = run_bass_kernel_spmd(nc, in_maps, core_ids=list(range(NCORES)))
    outp = np.zeros((SEQ, D), np.float32)
    for c in range(NCORES):
        o = np.asarray(res.results[c]["out"])
        for h in range(NH):
            sidx = c + 8 * h
            outp[sidx * T:(sidx + 1) * T] = o[h * T:(h + 1) * T]
    return outp.reshape(1, SEQ, D)
```
